# Optimizing a Trainium2 kernel written in Bass

```python
import jax, jax.numpy as jnp
from jax import lax
import numpy as np

D_MODEL = 2048
BATCH = 4
SEQ = 2048
DEPTH = 1
DEC_BATCH = 8
DEC_SEQ = 4
PAST_LEN = 16384
PAGE_SIZE = 128

HEAD_DIM = 128
CONV_DIM = D_MODEL // 4
CONV_W = 3
NSA_WIDTH = D_MODEL // 2
NSA_HEADS = NSA_WIDTH // HEAD_DIM
NSA_KV = 2
NSA_HPG = NSA_HEADS // NSA_KV
KV_WIDTH = NSA_KV * HEAD_DIM
MEM_HEADS = 4
MEM_WIDTH = MEM_HEADS * HEAD_DIM
N_MEM = 256
MIX_WIDTH = CONV_DIM + NSA_WIDTH + MEM_WIDTH
ROPE_DIM = HEAD_DIM // 4
ROPE_THETA = 500000.0
CMP_BLOCK = 64
SEL_BLOCK = 64
TOP_N = 16
WINDOW = 512
Q_BLOCK = 128
N_BRANCH = 3
N_KV_PARTS = 4
NORM_EPS = 1e-6
MASK_NEG = -1e30
FORCE = 1e9
ATTN_SCALE = HEAD_DIM ** -0.5

IN_SPLITS = (CONV_DIM, CONV_DIM, CONV_DIM, CONV_DIM,
             NSA_WIDTH, NSA_WIDTH, NSA_HEADS * N_BRANCH,
             KV_WIDTH * 6,
             MEM_WIDTH, MEM_WIDTH)
IN_WIDTH = sum(IN_SPLITS)

kernel_name = 'hymba_conv_nsa_mem_decode_step'


def rms_norm(x, g):
    xf = x.astype(jnp.float32)
    y = xf * lax.rsqrt(jnp.mean(xf * xf, axis=-1, keepdims=True) + NORM_EPS)
    return (y * g.astype(jnp.float32)).astype(x.dtype)


def rope(x, pos):
    half = ROPE_DIM // 2
    freqs = jnp.power(ROPE_THETA, -jnp.arange(half, dtype=jnp.float32) * (2.0 / ROPE_DIM))
    ang = pos.astype(jnp.float32)[:, None] * freqs[None, :]
    cos = jnp.cos(ang)[:, None, :]
    sin = jnp.sin(ang)[:, None, :]
    xf = x.astype(jnp.float32)
    x1 = xf[..., :half]
    x2 = xf[..., half:ROPE_DIM]
    out = jnp.concatenate([x1 * cos - x2 * sin, x2 * cos + x1 * sin, xf[..., ROPE_DIM:]], axis=-1)
    return out.astype(x.dtype)


def masked_softmax(s, mask, axis=-1):
    s = jnp.where(mask, s.astype(jnp.float32), MASK_NEG)
    p = jax.nn.softmax(s, axis=axis)
    return jnp.where(mask, p, 0.0)


def split_cols(z):
    idx = np.cumsum(IN_SPLITS)[:-1].tolist()
    return jnp.split(z, idx, axis=-1)


def gqa_attend(q, k, v, mask):
    B, T, H, D = q.shape
    G = k.shape[2]
    qg = q.reshape(B, T, G, H // G, D)
    s = jnp.einsum('btghd,bsgd->btghs', qg, k) * ATTN_SCALE
    if mask is None:
        p = jax.nn.softmax(s.astype(jnp.float32), axis=-1)
    else:
        p = masked_softmax(s, mask)
    o = jnp.einsum('btghs,bsgd->btghd', p.astype(v.dtype), v)
    return o.reshape(B, T, H, D)


def short_conv_mixer(h, b, c, conv_prev, w_conv):
    T = h.shape[1]
    u = c * h
    up = jnp.concatenate([conv_prev.astype(u.dtype), u], axis=1)
    y = w_conv[0] * up[:, 0:T]
    for j in range(1, CONV_W):
        y = y + w_conv[j] * up[:, j:j + T]
    return b * y, up[:, T:]


def compress(kv, a, w):
    B, L = kv.shape[:2]
    n = L // CMP_BLOCK
    blocks = kv[:, :n * CMP_BLOCK].reshape(B, n, CMP_BLOCK, NSA_KV, HEAD_DIM)
    pooled = jnp.einsum('bnjgd,jd->bngd', blocks, a)
    return jnp.einsum('bngd,de->bnge', pooled, w)


def nsa_cmp_sel(q, q_rot, q_pos, k_cmp, v_cmp, k_sel, v_sel):
    B, T = q.shape[:2]
    L = k_sel.shape[1]
    n_cmp = k_cmp.shape[1]
    n_sel = -(-L // SEL_BLOCK)
    top = min(TOP_N, n_sel)
    pad = n_sel * SEL_BLOCK - L
    padw = ((0, 0), (0, pad), (0, 0), (0, 0))
    kb = jnp.pad(k_sel, padw).reshape(B, n_sel, SEL_BLOCK, NSA_KV, HEAD_DIM).transpose(0, 3, 1, 2, 4)
    vb = jnp.pad(v_sel, padw).reshape(B, n_sel, SEL_BLOCK, NSA_KV, HEAD_DIM).transpose(0, 3, 1, 2, 4)
    cmp_end = (jnp.arange(n_cmp, dtype=jnp.int32) + 1) * CMP_BLOCK
    blk = jnp.arange(n_sel, dtype=jnp.int32)
    bidx = jnp.arange(B)[:, None, None, None]
    gidx = jnp.arange(NSA_KV)[None, :, None, None]
    qb = Q_BLOCK if T % Q_BLOCK == 0 else T
    nq = T // qb

    def one_block(args):
        qc, qrc, pc = args
        qg = qc.reshape(B, qb, NSA_KV, NSA_HPG, HEAD_DIM)
        s = jnp.einsum('btghd,bngd->btghn', qg, k_cmp) * ATTN_SCALE
        cmask = (cmp_end[None, :] <= pc[:, None] + 1)[None, :, None, None, :]
        p = masked_softmax(s, cmask)
        o_c = jnp.einsum('btghn,bngd->btghd', p.astype(v_cmp.dtype), v_cmp)
        imp = jnp.pad(p.sum(axis=3), ((0, 0), (0, 0), (0, 0), (0, n_sel - n_cmp)))
        cur = (pc // SEL_BLOCK)[:, None]
        forced = (blk[None] == 0) | (blk[None] == cur) | (blk[None] == cur - 1)
        imp = jnp.where(forced[None, :, None, :], FORCE, imp)
        imp = jnp.where((blk[None] > cur)[None, :, None, :], -1.0, imp)
        _, idx = lax.top_k(imp, top)
        idx = idx.transpose(0, 2, 1, 3)
        kg = kb[bidx, gidx, idx]
        vg = vb[bidx, gidx, idx]
        qrg = qrc.reshape(B, qb, NSA_KV, NSA_HPG, HEAD_DIM).transpose(0, 2, 1, 3, 4)
        s2 = jnp.einsum('bgthd,bgtkjd->bgthkj', qrg, kg) * ATTN_SCALE
        kpos = idx[..., None] * SEL_BLOCK + jnp.arange(SEL_BLOCK, dtype=jnp.int32)
        smask = (kpos <= pc[None, None, :, None, None])[:, :, :, None]
        p2 = masked_softmax(s2, smask, axis=(-2, -1))
        o_s = jnp.einsum('bgthkj,bgtkjd->bgthd', p2.astype(vg.dtype), vg)
        o_c = o_c.reshape(B, qb, NSA_HEADS, HEAD_DIM)
        o_s = o_s.transpose(0, 2, 1, 3, 4).reshape(B, qb, NSA_HEADS, HEAD_DIM)
        return o_c, o_s

    def split(a):
        return a.reshape(B, nq, qb, *a.shape[2:]).swapaxes(0, 1)

    o_c, o_s = lax.map(one_block, (split(q), split(q_rot), q_pos.reshape(nq, qb)))

    def merge(a):
        return a.swapaxes(0, 1).reshape(B, T, NSA_HEADS, HEAD_DIM)

    return merge(o_c), merge(o_s)


def window_banded(q_rot, k, v):
    B, T = q_rot.shape[:2]
    nb = T // Q_BLOCK
    nw = WINDOW // Q_BLOCK
    padw = ((0, 0), (WINDOW, 0), (0, 0), (0, 0))
    kp = jnp.pad(k, padw).reshape(B, nb + nw, Q_BLOCK, NSA_KV, HEAD_DIM)
    vp = jnp.pad(v, padw).reshape(B, nb + nw, Q_BLOCK, NSA_KV, HEAD_DIM)
    ks = jnp.concatenate([kp[:, j:j + nb] for j in range(nw + 1)], axis=2)
    vs = jnp.concatenate([vp[:, j:j + nb] for j in range(nw + 1)], axis=2)
    qpos = jnp.arange(T, dtype=jnp.int32).reshape(nb, Q_BLOCK)
    kpos = (jnp.arange(nb, dtype=jnp.int32)[:, None] * Q_BLOCK - WINDOW) + jnp.arange((nw + 1) * Q_BLOCK, dtype=jnp.int32)[None]
    kq = kpos[:, None, :]
    mask = (kq >= 0) & (kq <= qpos[:, :, None]) & (kq > qpos[:, :, None] - WINDOW)
    qg = q_rot.reshape(B, nb, Q_BLOCK, NSA_KV, NSA_HPG, HEAD_DIM)
    s = jnp.einsum('bnqghd,bnsgd->bnqghs', qg, ks) * ATTN_SCALE
    p = masked_softmax(s, mask[None, :, :, None, None, :])
    o = jnp.einsum('bnqghs,bnsgd->bnqghd', p.astype(vs.dtype), vs)
    return o.reshape(B, T, NSA_HEADS, HEAD_DIM)


def decoder_layer(x, past_kv, win_prev, conv_prev, mem_kv, norm_g, w_in, w_conv, a_cmp, w_cmp, w_out):
    B, T, _ = x.shape
    start = 0 if past_kv is None else past_kv.shape[1]
    pos = start + jnp.arange(T, dtype=jnp.int32)
    h = rms_norm(x, norm_g)
    c_h, c_b, c_c, c_g, q, n_g, b_g, kv6, m_q, m_g = split_cols(h @ w_in)

    y_a, conv_new = short_conv_mixer(c_h, c_b, c_c, conv_prev, w_conv)
    y_a = jax.nn.silu(c_g) * y_a

    q = q.reshape(B, T, NSA_HEADS, HEAD_DIM)
    q_rot = rope(q, pos)
    kv6 = kv6.reshape(B, T, 6, NSA_KV, HEAD_DIM)
    kv_new = jnp.stack([kv6[:, :, 0], kv6[:, :, 1], rope(kv6[:, :, 2], pos), kv6[:, :, 3]], axis=2)
    win_new = jnp.stack([rope(kv6[:, :, 4], pos), kv6[:, :, 5]], axis=2)
    kv_all = kv_new if past_kv is None else jnp.concatenate([past_kv.astype(kv_new.dtype), kv_new], axis=1)
    k_cmp = compress(kv_all[:, :, 0], a_cmp[0], w_cmp[0])
    v_cmp = compress(kv_all[:, :, 1], a_cmp[1], w_cmp[1])
    o_cmp, o_sel = nsa_cmp_sel(q, q_rot, pos, k_cmp, v_cmp, kv_all[:, :, 2], kv_all[:, :, 3])
    if win_prev is None:
        o_win = window_banded(q_rot, win_new[:, :, 0], win_new[:, :, 1])
        win_state = win_new[:, T - min(WINDOW, T):]
    else:
        win_all = jnp.concatenate([win_prev.astype(win_new.dtype), win_new], axis=1)
        S = win_all.shape[1]
        kpos = start + T - S + jnp.arange(S, dtype=jnp.int32)
        wmask = (kpos[None, :] <= pos[:, None]) & (kpos[None, :] > pos[:, None] - WINDOW)
        o_win = gqa_attend(q_rot, win_all[:, :, 0], win_all[:, :, 1], wmask[None, :, None, None, :])
        win_state = win_all[:, S - win_prev.shape[1]:]
    g = jax.nn.sigmoid(b_g.astype(jnp.float32)).reshape(B, T, NSA_HEADS, N_BRANCH).astype(x.dtype)
    o_nsa = g[..., 0:1] * o_cmp + g[..., 1:2] * o_sel + g[..., 2:3] * o_win
    y_b = jax.nn.silu(n_g) * o_nsa.reshape(B, T, NSA_WIDTH)

    o_m = gqa_attend(m_q.reshape(B, T, MEM_HEADS, HEAD_DIM), mem_kv[:, :, 0], mem_kv[:, :, 1], None)
    y_m = jax.nn.silu(m_g) * o_m.reshape(B, T, MEM_WIDTH)

    mixed = jnp.concatenate([y_a, y_b, y_m], axis=-1)
    return x + mixed @ w_out, kv_new, win_state, conv_new


def setup_inputs(seed: int = 0) -> dict:
    key = jax.random.key(seed)
    ks = jax.random.split(key, 18)
    f32 = jnp.float32
    n_pages = PAST_LEN // PAGE_SIZE
    n_pool = (DEC_BATCH * n_pages * 5) // 4
    w_buf = min(WINDOW, PAST_LEN)

    def nrm(k, shape, s=1.0):
        return jax.random.normal(k, shape, f32) * s

    perm = jax.random.permutation(ks[6], n_pool)
    return {
        'x_prompt': nrm(ks[0], (BATCH, SEQ, D_MODEL)),
        'x_sample': nrm(ks[1], (DEC_BATCH, DEC_SEQ, D_MODEL)),
        'cache_kv': nrm(ks[2], (DEPTH, n_pool, PAGE_SIZE, N_KV_PARTS, NSA_KV, HEAD_DIM)),
        'cache_win': nrm(ks[3], (DEPTH, DEC_BATCH, w_buf, 2, NSA_KV, HEAD_DIM)),
        'state_conv': nrm(ks[4], (DEPTH, DEC_BATCH, CONV_W - 1, CONV_DIM)),
        'cache_mem': nrm(ks[5], (DEPTH, DEC_BATCH, N_MEM, 2, MEM_HEADS, HEAD_DIM)),
        'page_table': perm[:DEC_BATCH * n_pages].reshape(DEC_BATCH, n_pages).astype(jnp.int32),
        'mem_prompt': nrm(ks[7], (BATCH, N_MEM, D_MODEL)),
        'norm_g': 1.0 + nrm(ks[8], (DEPTH, D_MODEL), 0.02),
        'w_in': nrm(ks[9], (DEPTH, D_MODEL, IN_WIDTH), D_MODEL ** -0.5),
        'w_conv': nrm(ks[10], (DEPTH, CONV_W, CONV_DIM), CONV_W ** -0.5),
        'a_cmp': nrm(ks[11], (DEPTH, 2, CMP_BLOCK, HEAD_DIM), CMP_BLOCK ** -0.5),
        'w_cmp': nrm(ks[12], (DEPTH, 2, HEAD_DIM, HEAD_DIM), HEAD_DIM ** -0.5),
        'mem_norm_g': 1.0 + nrm(ks[13], (DEPTH, D_MODEL), 0.02),
        'w_mem_kv': nrm(ks[14], (DEPTH, D_MODEL, 2 * MEM_WIDTH), D_MODEL ** -0.5),
        'w_out': nrm(ks[15], (DEPTH, MIX_WIDTH, D_MODEL), MIX_WIDTH ** -0.5),
        'final_g': 1.0 + nrm(ks[16], (D_MODEL,), 0.02),
    }


def reference(x_prompt, x_sample, cache_kv, cache_win, state_conv, cache_mem, page_table, mem_prompt,
              norm_g, w_in, w_conv, a_cmp, w_cmp, mem_norm_g, w_mem_kv, w_out, final_g):
    xp, xs = x_prompt, x_sample
    kv_p, win_p, conv_p, mem_p, kv_s, win_s, conv_s = [], [], [], [], [], [], []
    for l in range(DEPTH):
        lw = (norm_g[l], w_in[l], w_conv[l], a_cmp[l], w_cmp[l], w_out[l])
        mkv = (rms_norm(mem_prompt, mem_norm_g[l]) @ w_mem_kv[l]).reshape(
            mem_prompt.shape[0], N_MEM, 2, MEM_HEADS, HEAD_DIM)
        conv0 = jnp.zeros((xp.shape[0], CONV_W - 1, CONV_DIM), xp.dtype)
        xp, kvn, winn, convn = decoder_layer(xp, None, None, conv0, mkv, *lw)
        kv_p.append(kvn)
        win_p.append(winn)
        conv_p.append(convn)
        mem_p.append(mkv)
        past = cache_kv[l][page_table]
        past = past.reshape(past.shape[0], past.shape[1] * past.shape[2], *past.shape[3:])
        xs, kvn, winn, convn = decoder_layer(xs, past, cache_win[l], state_conv[l], cache_mem[l], *lw)
        kv_s.append(kvn)
        win_s.append(winn)
        conv_s.append(convn)
    y_prompt = rms_norm(xp, final_g)
    y_sample = rms_norm(xs, final_g)
    return (y_prompt, y_sample, jnp.stack(kv_p), jnp.stack(win_p), jnp.stack(conv_p), jnp.stack(mem_p),
            jnp.stack(kv_s), jnp.stack(win_s), jnp.stack(conv_s))
```

```python
import numpy as np
from contextlib import ExitStack
import concourse.bass as bass
import concourse.mybir as mybir
from concourse.bass_utils import run_bass_kernel_spmd

F32 = mybir.dt.float32
BF16 = mybir.dt.bfloat16
I32 = mybir.dt.int32
ALU = mybir.AluOpType
AF = mybir.ActivationFunctionType
AX = mybir.AxisListType

D = 2048
NX = 8
SCALE = 128 ** -0.5
EPS = 1e-6
NBLK = 53
O_CH, O_CB, O_CC, O_CG, O_Q, O_NG, O_BG, O_KV, O_MQ, O_MG = 0, 512, 1024, 1536, 2048, 3072, 4096, 4120, 5656, 6168


def block_list():
    bl = []
    for part in range(6):
        for g in range(2):
            bl.append(("kv", part, g, O_KV + part * 256 + g * 128))
    for cb in range(4):
        bl.append(("ch", cb, 0, O_CH + cb * 128))
        bl.append(("cc", cb, 0, O_CC + cb * 128))
        bl.append(("cb", cb, 0, O_CB + cb * 128))
        bl.append(("cg", cb, 0, O_CG + cb * 128))
    for h in range(8):
        bl.append(("q", h, 0, O_Q + h * 128))
    for h in range(8):
        bl.append(("ng", h, 0, O_NG + h * 128))
    bl.append(("bg", 0, 0, O_BG))
    for h in range(4):
        bl.append(("mq", h, 0, O_MQ + h * 128))
    for h in range(4):
        bl.append(("mg", h, 0, O_MG + h * 128))
    return bl


class Buf:
    __slots__ = ("w", "r", "x")

    def __init__(self, x=False):
        self.w = None
        self.r = {}
        self.x = x


class Sched:
    def __init__(self, nc, es):
        self.nc, self.es = nc, es
        self.eng = {"pe": nc.tensor, "act": nc.scalar, "dve": nc.vector, "pool": nc.gpsimd, "sp": nc.sync}
        self.semobj, self.sem, self.cnt, self.dmacnt = {}, {}, {}, {}
        self.waited = {e: {} for e in self.eng}
        for e in self.eng:
            self.sem[e] = self._reg(es.enter_context(nc.semaphore("q_" + e)))
            self.cnt[e] = 0

    def _reg(self, h):
        i = len(self.semobj)
        self.semobj[i] = h
        return i

    def newsem(self, name):
        i = self._reg(self.es.enter_context(self.nc.semaphore(name)))
        self.dmacnt[i] = 0
        return i

    def _deps(self, eng, reads, writes):
        d = {}

        def add(k, v):
            if eng == "pe" and k == self.sem["pe"]:
                return
            if d.get(k, 0) < v:
                d[k] = v

        for b in reads:
            if b.w is not None:
                add(*b.w)
            if b.x:
                for k, v in b.r.items():
                    if k != self.sem.get(eng):
                        add(k, v)
        for b in writes:
            if b.w is not None:
                add(*b.w)
            for k, v in b.r.items():
                add(k, v)
        w = self.waited[eng]
        for k, v in d.items():
            if w.get(k, 0) < v:
                self.eng[eng].wait_ge(self.semobj[k], v)
                w[k] = v

    def _mark(self, tok, reads, writes):
        for b in reads:
            if b.r.get(tok[0], 0) < tok[1]:
                b.r[tok[0]] = tok[1]
        for b in writes:
            b.w = tok
            b.r = {}

    def op(self, eng, fn, reads=(), writes=()):
        self._deps(eng, reads, writes)
        ins = fn(self.eng[eng])
        self.cnt[eng] += 1
        ins.then_inc(self.semobj[self.sem[eng]], 1)
        self._mark((self.sem[eng], self.cnt[eng]), reads, writes)

    def dma(self, q, sem, out, in_, reads=(), writes=(), indirect=None, **kw):
        self._deps(q, reads, writes)
        if indirect is None:
            ins = self.eng[q].dma_start(out=out, in_=in_, **kw)
        else:
            ins = self.eng[q].indirect_dma_start(out=out, out_offset=None, in_=in_,
                                                 in_offset=bass.IndirectOffsetOnAxis(ap=indirect, axis=0))
        self.dmacnt[sem] += 16
        ins.then_inc(self.semobj[sem], 16)
        self._mark((sem, self.dmacnt[sem]), reads, writes)

    def barrier(self):
        for e in self.eng:
            w = self.waited[e]
            for e2 in self.eng:
                k2, v2 = self.sem[e2], self.cnt[e2]
                if e2 != e and v2 > w.get(k2, 0):
                    self.eng[e].wait_ge(self.semobj[k2], v2)
                    w[k2] = v2
            for k2, v2 in self.dmacnt.items():
                if v2 > w.get(k2, 0):
                    self.eng[e].wait_ge(self.semobj[k2], v2)
                    w[k2] = v2

    def final_wait(self, eng="sp"):
        for k, v in self.dmacnt.items():
            if v > 0 and self.waited[eng].get(k, 0) < v:
                self.eng[eng].wait_ge(self.semobj[k], v)
        for e in self.eng:
            if self.cnt[e] > 0 and e != eng:
                self.eng[eng].wait_ge(self.semobj[self.sem[e]], self.cnt[e])


STOP = [99]


def build_program(n_pool):
    nc = bass.Bass("TRN2", target_bir_lowering=False)

    def din(name, shape, dt=F32):
        return nc.dram_tensor(name, list(shape), dt, kind="ExternalInput").ap()

    def dout(name, shape, dt=F32):
        return nc.dram_tensor(name, list(shape), dt, kind="ExternalOutput").ap()

    xall = din("xall", [2048, D])
    xext = din("xext", [NX, D])
    gn = din("gn", [128, D]); gm = din("gm", [128, D]); gf = din("gf", [128, D])
    win_t = din("win_t", [NBLK, 128, 16, 128])
    wm_t = din("wm_t", [4, 128, 16, 256])
    wo_t = din("wo_t", [2, 128, 16, 1024])
    memx = din("memx", [256, D])
    cosT = din("cosT", [32, 2048 + NX]); sinT = din("sinT", [32, 2048 + NX])
    identd = din("identd", [128, 128]); pmd = din("pmd", [32, 32])
    aTd = din("aTd", [128, 2, 64]); wcd = din("wcd", [128, 2, 128]); wcvd = din("wcvd", [128, 4, 3])
    stcd = din("stcd", [128, 4, 2])
    trid = din("trid", [128, 128]); winmd = din("winmd", [128, 8, 5, 128])
    cmpcd = din("cmpcd", [128, 8, 3, 32]); ealld = din("ealld", [32, 16, 128])
    cwin = din("cwin", [512, 512]); cmem = din("cmem", [256, 1024])
    cache2 = din("cache2", [n_pool * 256, 512])
    ptrepd = din("ptrepd", [128, 128], I32); ptcold = din("ptcold", [128, 1], I32)
    rowAd = din("rowAd", [128, 1]); colBd = din("colBd", [128, 128]); arepd = din("arepd", [128, 512])
    ind2d = din("ind2d", [128, 2]); rseld = din("rseld", [16, 4]); maskw16d = din("maskw16d", [128, 4, 16]); maskn16d = din("maskn16d", [4, 16])
    rselTd = din("rselTd", [4, 16]); onehd = din("onehd", [16, 6, 24])

    y_own = dout("y_own", [1024, D]); y_s = dout("y_s", [4, D])
    kv_own = dout("kv_own", [1024, 1024]); win_own = dout("win_own", [512, 512])
    conv_p = dout("conv_p", [2, 512]); memkv = dout("memkv", [256, 1024])
    kv_s = dout("kv_s", [4, 1024]); win_s = dout("win_s", [512, 512]); conv_s = dout("conv_s", [2, 512])

    with ExitStack() as es0:
        S = Sched(nc, es0)
        cur = [es0]

        def sb(name, shape, dt=F32):
            return cur[0].enter_context(nc.sbuf_tensor(name, list(shape), dt))

        ps = [es0.enter_context(nc.psum_tensor("ps%d" % i, [128, 512], F32)) for i in range(8)]
        psB = [Buf(True) for _ in range(8)]

        def psb16(i):
            return ps[i][:, :].bitcast(BF16)

        csem = S.newsem("csem")

        def const(name, src, shape, dt=F32, q="sp"):
            t = sb(name, shape, dt)
            b = Buf()
            S.dma(q, S.newsem("c_" + name), t[:], src, writes=[b])
            return t, b

        def mm(out, lhsT, rhs, start, stop, R, W, skip=False):
            S.op("pe", lambda e: e.matmul(out, lhsT=lhsT, rhs=rhs, start=start, stop=stop, skip_group_check=skip),
                 reads=R, writes=W)

        def tr(out, in_, ident, R, W):
            S.op("pe", lambda e: e.transpose(out=out, in_=in_, identity=ident), reads=R, writes=W)

        def act(out, in_, func, R, W, **kw):
            S.op("act", lambda e: e.activation(out=out, in_=in_, func=func, **kw), reads=R, writes=W)

        def tt(out, in0, in1, op, R, W, eng="dve"):
            S.op(eng, lambda e: e.tensor_tensor(out=out, in0=in0, in1=in1, op=op), reads=R, writes=W)

        def tsc(out, in0, s1, s2, op0, op1, R, W, eng="dve"):
            if s2 is None:
                S.op(eng, lambda e: e.tensor_scalar(out, in0, s1, None, op0=op0), reads=R, writes=W)
            else:
                S.op(eng, lambda e: e.tensor_scalar(out, in0, s1, s2, op0=op0, op1=op1), reads=R, writes=W)

        def stt(out, in0, scalar, in1, op0, op1, R, W, eng="dve"):
            S.op(eng, lambda e: e.scalar_tensor_tensor(out=out, in0=in0, scalar=scalar, in1=in1, op0=op0, op1=op1),
                 reads=R, writes=W)

        def cp(out, in_, R, W, eng="dve"):
            S.op(eng, lambda e: e.tensor_copy(out, in_), reads=R, writes=W)

        identf, identfB = const("identf", identd, [128, 128])
        identb, identbB = const("identb", identd, [128, 128], BF16, "pool")
        mixedT = sb("hT_oth", [128, 16, 1024], BF16); mixB = Buf()
        mixs = sb("mixs", [128, 16, 4], BF16); mixsB = Buf()
        hTx = sb("hTx", [128, 16, NX], BF16); hTxB = Buf()
        st = sb("st", [128, 32, 4]); stB = [Buf() for _ in range(32)]
        arBs = sb("arBs", [128, 16, 4], BF16); arBsB = Buf()
        qsT = sb("qsT", [128, 8, 4], BF16); qsB = Buf()
        qrs = sb("qrs", [128, 8, 4], BF16); qrsB = Buf()
        kvsT = sb("kvsT", [128, 12, 4]); kvsB = Buf()
        bgTs = sb("bgTs", [8, 24]); bgTsB = Buf()
        kcTs = sb("kcTs", [128, 2, 256], BF16); vcs = sb("vcs", [128, 2, 2, 128], BF16); kcsB = Buf()
        rvr = [sb("rvr%d" % i, [128, 4]) for i in range(4)]; rvrB = [Buf() for _ in range(4)]
        n_rv = [0]
        xsem = [S.newsem("xs0"), S.newsem("xs1"), S.newsem("xs2")]
        wtsem = [S.newsem("wt%d" % i) for i in range(3)]
        stsem = [S.newsem("st%d" % i) for i in range(3)]
        msem = S.newsem("msem")

        esA = ExitStack(); esA.__enter__(); cur[0] = esA
        hT_own = sb("hT_own", [128, 16, 1024], BF16); hT_ownB = Buf()
        hT = [hT_own, mixedT]; hTB = [hT_ownB, mixB]
        QT = sb("QT", [128, 8, 1024], BF16); QTB = [Buf() for _ in range(8)]
        memKT = sb("memKT", [128, 4, 256], BF16); memKB = Buf()
        memV = sb("memV", [128, 2, 4, 130], BF16); memVB = Buf()

        esB = ExitStack(); esB.__enter__(); cur[0] = esB
        KT = sb("KT", [128, 2, 2, 2048], BF16); KTB = Buf()
        Vt = sb("Vt", [128, 16, 2, 2, 130], BF16); VtB = Buf()
        Sc = sb("Sc", [128, 8, 8, 32]); ScB = Buf()
        arB = sb("arB", [128, 8, 1024], BF16); arBB = [Buf() for _ in range(8)]
        bgT = sb("bgT", [128, 8, 24]); bgTB = Buf()
        pooledT = sb("pooledT", [128, 2, 2, 32]); pooledB = Buf()
        kcT = sb("kcT", [128, 2, 32], BF16); vcm = sb("vcm", [32, 2, 128], BF16); kcB = Buf()
        cmpc, cmpcB = const("cmpc", cmpcd, [128, 8, 3, 32])
        selTA = sb("selTA", [32, 16, 128], BF16); selTAB = [Buf() for _ in range(16)]
        ScU = [Buf() for _ in range(16)]
        S.op("dve", lambda e: e.memset(Vt[:, :, :, :, 128:130], 1.0), writes=[VtB])
        S.op("dve", lambda e: e.memset(memV[:, :, :, 128:130], 1.0), writes=[memVB])
        S.op("dve", lambda e: e.memset(mixs[:], 0.0), writes=[mixsB])

        e0 = ExitStack(); e0.__enter__(); cur[0] = e0
        gbc, gbcB = const("gbc", gn, [128, D])
        junk = sb("junk", [128, D], BF16)
        xs = [sb("xs0", [128, D]), sb("xs1", [128, D]), sb("xs2", [128, D])]; xsB = [Buf(), Buf(), Buf()]
        NXS = 3
        hb = [sb("hb0", [128, D], BF16), sb("hb1", [128, D], BF16)]; hbB = [Buf(), Buf()]

        def norm_stats(i, x_ap, xB_, npart):
            act(junk[:npart, :], x_ap, AF.Square, [xB_], [stB[i]], accum_out=st[:npart, i, 0:1])
            tsc(st[:npart, i, 1:2], st[:npart, i, 0:1], 1.0 / D, EPS, ALU.mult, ALU.add, [stB[i]], [stB[i]])
            act(st[:npart, i, 2:3], st[:npart, i, 1:2], AF.Sqrt, [stB[i]], [stB[i]])
            S.op("dve", lambda e: e.reciprocal(out=st[:npart, i, 3:4], in_=st[:npart, i, 2:3]),
                 reads=[stB[i]], writes=[stB[i]])

        def norm_load(i, src_ap, npart):
            j = i % NXS
            S.dma("sp", xsem[j], xs[j][:npart, :], src_ap, writes=[xsB[j]])
            norm_stats(i, xs[j][:npart, :], xsB[j], npart)

        def norm_rest(i, npart, dst_fn, dstB, nxt=None):
            j = i % 2
            jx = i % NXS
            stt(hb[j][:npart, :], xs[jx][:npart, :], st[:npart, i, 3:4], gbc[:npart, :], ALU.mult, ALU.mult,
                [xsB[jx], stB[i], gbcB], [hbB[j]])
            if nxt is not None:
                nxt()
            for half in range(2):
                bank = 6 + half
                for kk in range(8):
                    k = half * 8 + kk
                    tr(psb16(bank)[:, kk * 128:kk * 128 + npart], hb[j][:npart, k * 128:(k + 1) * 128],
                       identb[:npart, :npart], [hbB[j], identbB], [psB[bank]])
                src3 = psb16(bank).rearrange("p (k t) -> p k t", k=8)[:, :, 0:npart]
                if half == 0:
                    act(dst_fn(half), src3, AF.Copy, [psB[bank]], [dstB])
                else:
                    cp(dst_fn(half), src3, [psB[bank]], [dstB])

        def norm_tile(i, src_ap, npart, dst_fn, dstB):
            norm_load(i, src_ap, npart)
            norm_rest(i, npart, dst_fn, dstB)

        jobs = []
        for i in range(16):
            hh, t0 = i // 8, (i % 8) * 128
            jobs.append((i, xall[i * 128:(i + 1) * 128, :], 128,
                         (lambda half, hh=hh, t0=t0: hT[hh][:, half * 8:(half + 1) * 8, t0:t0 + 128]), hTB[hh]))
        jobs.append((16, xext[:, :], NX, (lambda half: hTx[:, half * 8:(half + 1) * 8, :]), hTxB))
        norm_load(jobs[0][0], jobs[0][1], jobs[0][2])
        norm_load(jobs[1][0], jobs[1][1], jobs[1][2])
        for n_, (i, src_ap, npart, dst_fn, dstB) in enumerate(jobs):
            nxt = None
            if n_ + 2 < len(jobs):
                ni, nsrc, nnp = jobs[n_ + 2][0], jobs[n_ + 2][1], jobs[n_ + 2][2]
                nxt = (lambda ni=ni, nsrc=nsrc, nnp=nnp: norm_load(ni, nsrc, nnp))
            norm_rest(i, npart, dst_fn, dstB, nxt)

        S.dma("sp", csem, gbc[:], gm, writes=[gbcB])
        hmT = sb("hmT", [128, 16, 256], BF16); hmTB = Buf()
        for i in range(2):
            norm_tile(17 + i, memx[i * 128:(i + 1) * 128, :], 128,
                      lambda half, i=i: hmT[:, half * 8:(half + 1) * 8, i * 128:(i + 1) * 128], hmTB)
        wmr0_ = xs[1][:, :].bitcast(BF16).rearrange("p (k c) -> p k c", k=16); wmr = [wmr0_, wmr0_]; wmrB = [xsB[1], xsB[1]]
        mst0_ = sb("mst0", [128, 256]); mst = [mst0_, mst0_]; mst0B_ = Buf(); mstB = [mst0B_, mst0B_]
        mkb = sb("mkb", [128, 256], BF16); mkbB = Buf()
        n_m = 0
        for cg in range(4 if STOP[0] > -10 else 1):
            j = cg % 2
            S.dma("pool", wtsem[0], wmr[j], wm_t[cg], writes=[wmrB[j]])
            for mt in range(2):
                bank = mt
                for k in range(16):
                    mm(ps[bank][:, 0:256], hmT[:, k, mt * 128:(mt + 1) * 128], wmr[j][:, k, :], k == 0, k == 15,
                       [hmTB, wmrB[j]], [psB[bank]])
                jj = n_m % 2
                n_m += 1
                act(mst[jj][:], ps[bank][:, 0:256], AF.Copy, [psB[bank]], [mstB[jj]])
                if STOP[0] != -21:
                    S.dma("sp", stsem[0], memkv[mt * 128:(mt + 1) * 128, cg * 256:(cg + 1) * 256], mst[jj][:],
                          reads=[mstB[jj]])
                if STOP[0] == -31:
                    continue
                if cg < 2:
                    cp(mkb[:], ps[bank][:, 0:256], [psB[bank]], [mkbB])
                    for h2 in range(2):
                        tr(psb16(6)[:, h2 * 128:(h2 + 1) * 128], mkb[:, h2 * 128:(h2 + 1) * 128], identb[:],
                           [mkbB, identbB], [psB[6]])
                    cp(memKT[:, 2 * cg:2 * cg + 2, mt * 128:(mt + 1) * 128],
                       psb16(6)[:, 0:256].rearrange("p (h t) -> p h t", h=2), [psB[6]], [memKB])
                else:
                    cp(memV[:, mt, 2 * (cg - 2):2 * (cg - 2) + 2, 0:128],
                       ps[bank][:, 0:256].rearrange("p (h t) -> p h t", h=2), [psB[bank]], [memVB])
        S.barrier()
        e0.close()
        if STOP[0] == 0 or STOP[0] <= -10:
            S.final_wait("sp")
            esB.close(); esA.close()
            return nc

        e1 = ExitStack(); e1.__enter__(); cur[0] = e1
        pmf, pmfB = const("pmf", pmd, [32, 32])
        cosS, cosB = const("cosS", cosT, [32, 2048 + NX], BF16, "pool")
        sinS, sinB = const("sinS", sinT, [32, 2048 + NX], BF16, "pool")
        aT, aTB = const("aT", aTd, [128, 2, 64])
        wc, wcB = const("wc", wcd, [128, 2, 128])
        wcv, wcvB = const("wcv", wcvd, [128, 4, 3])
        stc, stcB = const("stc", stcd, [128, 4, 2])
        wt = [sb("wt%d" % i, [128, 16, 128], BF16) for i in range(3)]; wtB = [Buf() for _ in range(3)]
        kf = [sb("kf0", [128, 1024 + NX]), sb("kf1", [128, 1024 + NX])]; kfB = [Buf(), Buf()]
        tmpf = sb("tmpf", [128, 1024 + NX]); tmpB = Buf()
        vb16 = sb("vb16", [128, 1024], BF16); vb16B = Buf()
        ost = [sb("ost%d" % i, [128, 4, 128]) for i in range(2)]; ostB = [Buf() for _ in range(2)]
        uext = sb("uext", [128, 1024 + 2]); uextB = Buf()
        usx = sb("usx", [128, 8]); usxB = Buf()
        chf = sb("chf", [128, 1024 + NX]); chfB = Buf()
        tmp2 = chf; tmp2B = chfB
        cvo = sb("cvo", [128, 2, 4, 2]); cvoB = Buf()
        bgf = chf; bgfB = chfB
        n_ost = [0]; bankset = [0]; nkf = [0]
        e4 = sb("e4", [128, 4, 32]); e4B = Buf()
        sm = sb("sm", [128, 16]); smB = Buf()
        imp = sb("imp", [128, 3, 32]); impB = Buf()
        m8 = sb("m8", [128, 16]); m8B = Buf()

        def cmp_prepass(s, g):
            u = s * 2 + g
            pc = Sc[:, s, 4 * g:4 * g + 4, :]
            tt(e4[:], pc, cmpc[:, s, 0, :].unsqueeze(1).to_broadcast([128, 4, 32]), ALU.add, [ScB, cmpcB], [e4B])
            act(e4[:], e4[:], AF.Exp, [e4B], [e4B], scale=SCALE)
            S.op("dve", lambda e: e.tensor_reduce(out=sm[:, 0:4], in_=e4[:], axis=AX.X, op=ALU.add), reads=[e4B], writes=[smB])
            tsc(sm[:, 0:4], sm[:, 0:4], 1e-30, None, ALU.add, None, [smB], [smB])
            S.op("dve", lambda e: e.reciprocal(out=sm[:, 4:8], in_=sm[:, 0:4]), reads=[smB], writes=[smB])
            tt(pc, e4[:], sm[:, 4:8].unsqueeze(2).to_broadcast([128, 4, 32]), ALU.mult, [e4B, smB, ScB], [ScU[u]])
            S.op("dve", lambda e: e.tensor_reduce(out=imp[:, 0, :], in_=pc.rearrange("p h m -> p m h"), axis=AX.X,
                                                  op=ALU.add), reads=[ScU[u]], writes=[impB])
            tt(imp[:, 0, :], imp[:, 0, :], cmpc[:, s, 1, :], ALU.mult, [impB, cmpcB], [impB])
            tt(imp[:, 0, :], imp[:, 0, :], cmpc[:, s, 2, :], ALU.add, [impB, cmpcB], [impB])
            S.op("dve", lambda e: e.max(out=m8[:, 0:8], in_=imp[:, 0, :]), reads=[impB], writes=[m8B])
            S.op("dve", lambda e: e.match_replace(out=imp[:, 1, :], in_to_replace=m8[:, 0:8], in_values=imp[:, 0, :],
                                                  imm_value=-2.0), reads=[impB, m8B], writes=[impB])
            S.op("dve", lambda e: e.max(out=m8[:, 8:16], in_=imp[:, 1, :]), reads=[impB], writes=[m8B])
            tsc(imp[:, 2, :], imp[:, 0, :], m8[:, 15:16], None, ALU.is_ge, None, [impB, m8B], [impB])
            tr(ps[7][0:32, 0:128], imp[:, 2, :], identf[:, :], [impB, identfB], [psB[7]])
            cp(selTA[:, u, :], ps[7][0:32, 0:128], [psB[7]], [selTAB[u]])


        def out_rows(src, srcB, dst):
            S_ = n_ost[0] % 2
            n_ost[0] += 1
            ob_ = 6 + S_
            for t4 in range(4):
                tr(ps[ob_][:, t4 * 128:(t4 + 1) * 128], src[:, t4 * 128:(t4 + 1) * 128], identf[:],
                   [srcB, identfB], [psB[ob_]])
            act(ost[S_][:], ps[ob_][:, :].rearrange("p (t d) -> p t d", t=4), AF.Copy, [psB[ob_]], [ostB[S_]])
            S.dma("sp", stsem[S_], dst, ost[S_][:], reads=[ostB[S_]])

        def rope(buf, bufB, ncols, tabcol0):
            for c0 in range(0, ncols, 512):
                n = min(512, ncols - c0)
                mm(ps[6][0:32, 0:n], pmf[:, :], buf[0:32, c0:c0 + n], True, True, [bufB, pmfB], [psB[6]])
                tt(tmp2[0:32, c0:c0 + n], ps[6][0:32, 0:n], sinS[:, tabcol0 + c0:tabcol0 + c0 + n], ALU.mult,
                   [psB[6], sinB], [tmp2B])
            tt(buf[0:32, 0:ncols], buf[0:32, 0:ncols], cosS[:, tabcol0:tabcol0 + ncols], ALU.mult, [bufB, cosB], [bufB])
            tt(buf[0:32, 0:ncols], buf[0:32, 0:ncols], tmp2[0:32, 0:ncols], ALU.add, [bufB, tmp2B], [bufB])

        blocks = block_list()
        pend_epi = []

        def epilogue(kind, a0, a1, half, b0, b1, bx):
            def evac(dst, dstB, func=AF.Copy, rows=128, withx=True):
                act(dst[0:rows, 0:512], ps[b0][0:rows, :], func, [psB[b0]], [dstB])
                act(dst[0:rows, 512:1024], ps[b1][0:rows, :], func, [psB[b1]], [dstB])
                if withx:
                    act(dst[0:rows, 1024:1024 + NX], ps[bx][0:rows, 0:NX], func, [psB[bx]], [dstB])

            if kind == "kv":
                part, g = a0, a1
                kfi = nkf[0] % 2
                nkf[0] += 1
                K_, KB_ = kf[kfi], kfB[kfi]
                evac(K_, KB_, withx=(half == 0))
                if part in (2, 4):
                    if half == 0:
                        rope(K_, KB_, 1024 + NX, 0)
                    else:
                        rope(K_, KB_, 1024, 1024 + NX)
                    act(KT[:, (part - 2) // 2, g, half * 1024:(half + 1) * 1024], K_[:, 0:1024], AF.Copy, [KB_], [KTB])
                if part in (3, 5):
                    cp(vb16[:], K_[:, 0:1024], [KB_], [vb16B])
                    for t8 in range(8):
                        tr(psb16(7)[:, t8 * 128:(t8 + 1) * 128], vb16[:, t8 * 128:(t8 + 1) * 128], identb[:],
                           [vb16B, identbB], [psB[7]])
                    cp(Vt[:, half * 8:(half + 1) * 8, (part - 3) // 2, g, 0:128],
                       psb16(7).rearrange("p (t d) -> p t d", t=8), [psB[7]], [VtB])
                if part in (0, 1):
                    for c0 in (0, 512):
                        tt(tmpf[:, c0:c0 + 512].rearrange("p (n j) -> p n j", j=64),
                           K_[:, c0:c0 + 512].rearrange("p (n j) -> p n j", j=64),
                           aT[:, part, :].unsqueeze(1).to_broadcast([128, 8, 64]), ALU.mult, [KB_, aTB], [tmpB])
                    S.op("dve", lambda e: e.tensor_reduce(
                        out=pooledT[:, part, g, half * 16:(half + 1) * 16],
                        in_=tmpf[:, 0:1024].rearrange("p (n j) -> p n j", j=64), axis=AX.X, op=ALU.add),
                         reads=[tmpB], writes=[pooledB])
                if half == 0:
                    cp(kvsT[:, part * 2 + g, :], K_[:, 1026:1030], [KB_], [kvsB])
                    cidx = part * 2 + g
                    if part < 4:
                        for c0 in (0, 512):
                            out_rows(K_[:, c0:c0 + 512], KB_,
                                     kv_own[c0:c0 + 512, cidx * 128:(cidx + 1) * 128].rearrange("(t p) d -> p t d", p=128))
                    else:
                        out_rows(K_[:, 512:1024], KB_,
                                 win_own[:, (cidx - 8) * 128:(cidx - 7) * 128].rearrange("(t p) d -> p t d", p=128))
            elif kind == "ch":
                evac(chf, chfB)
            elif kind == "cc":
                cb_ = a0
                for c0, bank in ((0, b0), (512, b1)):
                    tt(uext[:, 2 + c0:2 + c0 + 512], ps[bank][:, :], chf[:, c0:c0 + 512], ALU.mult,
                       [psB[bank], chfB], [uextB])
                tt(usx[:, 0:NX], ps[bx][:, 0:NX], chf[:, 1024:1024 + NX], ALU.mult, [psB[bx], chfB], [usxB])
                cp(uext[:, 0:2], usx[:, 0:2], [usxB], [uextB])
                cp(usx[:, 0:2], stc[:, cb_, :], [stcB, uextB], [usxB])
                cp(cvo[:, 0, cb_, :], uext[:, 1024:1026], [uextB], [cvoB])
                cp(cvo[:, 1, cb_, :], usx[:, 4:6], [usxB], [cvoB])
                for (u_, uB_, n_, o_) in ((uext, uextB, 1024, 0), (usx, usxB, 4, 1024)):
                    tsc(tmpf[:, o_:o_ + n_], u_[:, 0:n_], wcv[:, cb_, 0:1], None, ALU.mult, None, [uB_, wcvB], [tmpB])
                    for jj in (1, 2):
                        stt(tmpf[:, o_:o_ + n_], u_[:, jj:jj + n_], wcv[:, cb_, jj:jj + 1], tmpf[:, o_:o_ + n_],
                            ALU.mult, ALU.add, [uB_, wcvB, tmpB], [tmpB])
            elif kind == "cb":
                for c0, bank in ((0, b0), (512, b1)):
                    tt(tmpf[:, c0:c0 + 512], tmpf[:, c0:c0 + 512], ps[bank][:, :], ALU.mult, [tmpB, psB[bank]], [tmpB])
                tt(tmpf[:, 1024:1028], tmpf[:, 1024:1028], ps[bx][:, 2:6], ALU.mult, [tmpB, psB[bx]], [tmpB])
            elif kind == "cg":
                cb_ = a0
                evac(chf, chfB, func=AF.Silu)
                tt(mixedT[:, cb_, :], tmpf[:, 0:1024], chf[:, 0:1024], ALU.mult, [tmpB, chfB], [mixB])
                tt(mixs[:, cb_, :], tmpf[:, 1024:1028], chf[:, 1026:1030], ALU.mult, [tmpB, chfB], [mixsB])
            elif kind == "q":
                h = a0
                g = h // 4
                kfi = nkf[0] % 2
                nkf[0] += 1
                K_, KB_ = kf[kfi], kfB[kfi]
                evac(K_, KB_)
                act(QT[:, h, :], K_[:, 0:1024], AF.Copy, [KB_], [QTB[h]])
                act(qsT[:, h, :], K_[:, 1026:1030], AF.Copy, [KB_], [qsB])
                for s in range(8):
                    mm(ps[7][:, s * 32:(s + 1) * 32], QT[:, h, s * 128:(s + 1) * 128], kcT[:, g, :], True, True,
                       [QTB[h], kcB], [psB[7]])
                cp(Sc[:, :, h, :], ps[7][:, 0:256].rearrange("p (s m) -> p s m", s=8), [psB[7]], [ScB])
                rope(K_, KB_, 1024 + NX, 0)
                act(QT[0:32, h, :], K_[0:32, 0:1024], AF.Copy, [KB_], [QTB[h]])
                act(qrs[:, h, :], K_[:, 1026:1030], AF.Copy, [KB_], [qrsB])
            elif kind == "ng":
                h = a0
                act(arB[:, h, 0:512], ps[b0][:, :], AF.Silu, [psB[b0]], [arBB[h]])
                act(arB[:, h, 512:1024], ps[b1][:, :], AF.Silu, [psB[b1]], [arBB[h]])
                act(arBs[:, h, :], ps[bx][:, 2:6], AF.Silu, [psB[bx]], [arBsB])
                cmp_prepass(h, 0)
                cmp_prepass(h, 1)
            elif kind == "bg":
                evac(bgf, bgfB, func=AF.Sigmoid, rows=24)
                for s in range(8):
                    tr(ps[7][:, s * 24:(s + 1) * 24], bgf[0:24, s * 128:(s + 1) * 128], identf[0:24, 0:24],
                       [bgfB, identfB], [psB[7]])
                cp(bgT[:, :, :], ps[7][:, 0:192].rearrange("p (s c) -> p s c", s=8), [psB[7]], [bgTB])
                tr(ps[7][0:4, 256:280], bgf[0:24, 1026:1030], identf[0:24, 0:24], [bgfB, identfB], [psB[7]])
                cp(bgTs[0:4, :], ps[7][0:4, 256:280], [psB[7]], [bgTsB])

            if kind == "kv" and a0 == 1 and a1 == 1 and half == 1:
                for g in range(2):
                    mm(ps[6][:, 0:32], wc[:, 0, :], pooledT[:, 0, g, :], True, True, [wcB, pooledB], [psB[6]])
                    cp(kcT[:, g, :], ps[6][:, 0:32], [psB[6]], [kcB])
                    mm(ps[6][0:32, 0:128], pooledT[:, 1, g, :], wc[:, 1, :], True, True, [wcB, pooledB], [psB[6]])
                    cp(vcm[:, g, :], ps[6][0:32, 0:128], [psB[6]], [kcB])


        for bi, (kind, a0, a1, col) in enumerate(blocks):
            if kind in ("mq", "mg"):
                continue
            ws = bi % 3
            S.dma("pool", wtsem[ws], wt[ws][:], win_t[bi], writes=[wtB[ws]])
            passes = [0, 1] if kind == "kv" else [0]
            for half in passes:
                bs = bankset[0]
                bankset[0] = 3 - bs
                b0, b1, bx = bs, bs + 1, bs + 2
                for k in range(16):
                    for gi, bank in ((0, b0), (1, b1)):
                        mm(ps[bank][:, :], wt[ws][:, k, :], hT[half][:, k, gi * 512:(gi + 1) * 512], k == 0, k == 15,
                           [wtB[ws], hTB[half]], [psB[bank]])
                    if half == 0:
                        mm(ps[bx][:, 0:NX], wt[ws][:, k, :], hTx[:, k, :], k == 0, k == 15, [wtB[ws], hTxB], [psB[bx]])

                if pend_epi:
                    epilogue(*pend_epi.pop(0))
                pend_epi.append((kind, a0, a1, half, b0, b1, bx))
        while pend_epi:
            epilogue(*pend_epi.pop(0))
        with nc.allow_non_contiguous_dma(reason="tiny transposed conv-state store"):
            for cb_ in range(4):
                S.dma("sp", msem, conv_p[:, cb_ * 128:(cb_ + 1) * 128].rearrange("j p -> p j"), cvo[:, 0, cb_, :], reads=[cvoB])
                S.dma("sp", msem, conv_s[:, cb_ * 128:(cb_ + 1) * 128].rearrange("j p -> p j"), cvo[:, 1, cb_, :], reads=[cvoB])
        for grp in range(3):
            for b4 in range(4):
                tr(ps[7][0:4, b4 * 128:(b4 + 1) * 128], kvsT[:, grp * 4 + b4, :], identf[:, :], [kvsB, identfB], [psB[7]])
            cp(tmpf[0:4, 0:512], ps[7][0:4, :], [psB[7]], [tmpB])
            dsts = kv_s[:, grp * 512:(grp + 1) * 512] if grp < 2 else win_s[508:512, :]
            S.dma("sp", msem, dsts, tmpf[0:4, 0:512], reads=[tmpB])
        S.dma("sp", msem, win_s[0:508, :], cwin[4:512, :])
        S.barrier()
        e1.close()
        if STOP[0] == 1:
            S.final_wait("sp")
            esB.close(); esA.close()
            return nc

        e2 = ExitStack(); e2.__enter__(); cur[0] = e2
        tri, triB = const("tri", trid, [128, 128], BF16, "pool")
        winm, winmB = const("winm", winmd, [128, 8, 5, 128], BF16, "pool")
        eall, eallB = const("eall", ealld, [32, 16, 128], BF16, "pool")
        PT = [sb("PT0", [128, 512], BF16), sb("PT1", [128, 512], BF16)]; PTB = [Buf(), Buf()]
        n_att = [0]
        wc2, wc2B = const("wc2", wcd, [128, 2, 128])
        arep, arepB = const("arep", arepd, [128, 512])
        ind2, ind2B = const("ind2", ind2d, [128, 2], BF16, "pool")
        ptrep, ptrepB = const("ptrep", ptrepd, [128, 128], I32)
        rowA, rowAB = const("rowA", rowAd, [128, 1])
        idxfA = sb("idxfA", [128, 128]); idxfAB = Buf()
        idxA = sb("idxA", [128, 128], I32); idxAB = Buf()
        cp(idxfA[:], ptrep[:], [ptrepB], [idxfAB])
        tsc(idxfA[:], idxfA[:], 256.0, rowA[:, 0:1], ALU.mult, ALU.add, [idxfAB, rowAB], [idxfAB])
        cp(idxA[:], idxfA[:], [idxfAB], [idxAB])
        NR = 4
        ct = [sb("ct%d" % i, [128, 512]) for i in range(NR)]; ctB = [Buf() for _ in range(NR)]
        ctsem = [S.newsem("ct%d" % i) for i in range(NR)]
        prod = [sb("prod%d" % i, [128, 512], BF16) for i in range(2)]; prodB = [Buf(), Buf()]
        pl = sb("pl", [128, 4, 128, 2]); plB = Buf()
        p1 = dict(g=0, m=0, p=0)

        def pass1_step():
            if p1["g"] < 128 and p1["g"] - p1["p"] < NR:
                s_ = p1["g"]
                S.dma("pool", ctsem[s_ % NR], ct[s_ % NR][:], cache2, reads=[idxAB], writes=[ctB[s_ % NR]],
                      indirect=idxA[:, s_:s_ + 1])
                p1["g"] += 1
            if p1["m"] < p1["g"] - 1 or (p1["g"] == 128 and p1["m"] < 128):
                s_ = p1["m"]
                tt(prod[s_ % 2][:], ct[s_ % NR][:], arep[:], ALU.mult, [ctB[s_ % NR], arepB], [prodB[s_ % 2]])
                p1["m"] += 1
            if p1["p"] < p1["m"] - 1 or (p1["m"] == 128 and p1["p"] < 128):
                s_ = p1["p"]
                for pg in range(4):
                    c0 = ((s_ % 64) * 4 + pg) * 2
                    mm(ps[7][:, c0:c0 + 2], prod[s_ % 2][:, pg * 128:(pg + 1) * 128], ind2[:, :], True, True,
                       [prodB[s_ % 2], ind2B], [psB[7]])
                if s_ % 64 == 63:
                    s0 = s_ - 63
                    cp(pl[:, :, s0:s0 + 64, :], ps[7][:, :].rearrange("p (s g c) -> p g s c", s=64, g=4), [psB[7]], [plB])
                p1["p"] += 1


        pend = []
        p1_on = [False]
        sbase = [4]

        def att_B(nh, nq, i, t, ob, first, last):
            nk = t["nk"]
            for h in range(nh):
                bank = ob[h // 2]
                off = t.get("ooff", (h % 2) * 130)
                mm(ps[bank][0:nq, off:off + 129], PT[i][0:nk, h * nq:(h + 1) * nq], t["va"],
                   first and h % 2 == 0, last, [PTB[i]] + t["vR"], [psB[bank]], skip=True)

        after = []

        def att_flush():
            while pend:
                att_B(*pend.pop(0))
            while after:
                after.pop(0)()

        def att_tile(nh, nq, qrhs, qR, t, ob, first, last):
            if callable(t):
                t = t()
            W = nh * nq
            i = n_att[0] % 2
            n_att[0] += 1
            nk = t["nk"]
            m = None
            if t.get("mask") is not None:
                m = t["mask"]() if callable(t["mask"]) else t["mask"]
            sb_ = sbase[0] + i
            so = ps[sb_][0:nk, 0:W]
            if len(qrhs.shape) == 3:
                so = so.rearrange("p (h q) -> p h q", h=nh)
            mm(so, t["kt"], qrhs, True, True, t["kR"] + qR, [psB[sb_]])
            att_flush()
            act(PT[i][0:nk, 0:W], ps[sb_][0:nk, 0:W], AF.Exp, [psB[sb_]], [PTB[i]], scale=SCALE)
            if m is not None:
                m_ap, mR = m
                p3 = PT[i][0:nk, 0:W].rearrange("p (h q) -> p h q", h=nh)
                tt(p3, p3, m_ap.unsqueeze(1).to_broadcast([nk, nh, nq]), ALU.mult, [PTB[i]] + mR, [PTB[i]])
            pend.append((nh, nq, i, t, ob, first, last))
            if p1_on[0]:
                pass1_step()

        def attend(nh, nq, qrhs, qR, tiles, ob, flush=True):
            nt = len(tiles)
            for ti, t in enumerate(tiles):
                att_tile(nh, nq, qrhs, qR, t, ob, ti == 0, ti == nt - 1)
            if flush:
                att_flush()

        pT = sb("pT", [32, 4, 128], BF16); pTB = Buf()
        onsa = [sb("onsa0", [128, 4, 128]), sb("onsa1", [128, 4, 128])]; onsaB = [[Buf() for _ in range(4)] for _ in range(2)]
        onb = sb("onb", [128, 4, 128], BF16); onbB = Buf()
        mk = [sb("mk0", [128, 128], BF16), sb("mk1", [128, 128], BF16)]; mkB = [Buf(), Buf()]
        rv = sb("rv", [128, 8]); rvB = Buf()
        n_mk = [0]
        mkps = [Buf(), Buf()]


        def fin_branch(ob, on, onBh, gate2_fn, nq=128):
            gR = [bgTsB] + ([bgTB] if nq == 128 else [])
            rs_ = []
            for hp in range(2):
                ri = n_rv[0] % 4
                n_rv[0] += 1
                rs_.append((rvr[ri], rvrB[ri], ps[ob[hp]][0:nq, 0:260].rearrange("p (h c) -> p h c", h=2)))
            for hp, (R_, RB_, o3) in enumerate(rs_):
                S.op("dve", lambda e: e.reciprocal(out=R_[0:nq, 0:2], in_=o3[:, :, 128]), reads=[psB[ob[hp]]], writes=[RB_])
            for hp, (R_, RB_, o3) in enumerate(rs_):
                tt(R_[0:nq, 2:4], R_[0:nq, 0:2], gate2_fn(hp), ALU.mult, [RB_] + gR, [RB_])
            for hp, (R_, RB_, o3) in enumerate(rs_):
                for hh in range(2):
                    h = 2 * hp + hh
                    stt(on[0:nq, h, :], o3[:, hh, 0:128], R_[0:nq, 2 + hh:3 + hh], on[0:nq, h, :],
                        ALU.mult, ALU.add, [psB[ob[hp]], RB_, onBh[h]], [onBh[h]])

        p1_on[0] = True
        for s in range(8):
            for g in range(2):
                oi = (s * 2 + g) % 2
                on, onB_ = onsa[oi], onsaB[oi]
                bg3 = bgT[:, s, :].rearrange("p (h b) -> p h b", b=3)
                u = s * 2 + g
                pc = Sc[:, s, 4 * g:4 * g + 4, :]
                pcB = ScU[u]
                for h in range(4):
                    tr(ps[6][0:32, h * 128:(h + 1) * 128], pc[:, h, :], identf[:, :], [pcB, identfB], [psB[6]])
                cp(pT[:], ps[6][0:32, :].rearrange("p (h q) -> p h q", h=4), [psB[6]], [pTB])
                for h in range(4):
                    mm(ps[6][:, h * 128:(h + 1) * 128], pT[:, h, :], vcm[:, g, :], True, True, [pTB, kcB], [psB[6]])
                tt(on[:], ps[6][:, :].rearrange("p (h d) -> p h d", h=4),
                   bg3[:, 4 * g:4 * g + 4, 0].unsqueeze(2).to_broadcast([128, 4, 128]), ALU.mult, [psB[6], bgTB], onB_)
                qrhs = QT[:, 4 * g:4 * g + 4, s * 128:(s + 1) * 128]
                qR = [QTB[4 * g + h] for h in range(4)]
                def selmask(j):
                    def f():
                        mi = n_mk[0] % 2
                        n_mk[0] += 1
                        reg = ps[6][:, 0:128]
                        mm(reg, eall[:, j, :], selTA[:, u, :], True, True, [eallB, selTAB[u]], [psB[6]])
                        if j == s:
                            tt(mk[mi][:], reg, tri[:], ALU.mult, [psB[6], triB], [mkB[mi]])
                        else:
                            act(mk[mi][:], reg, AF.Copy, [psB[6]], [mkB[mi]])
                        return mk[mi][:], [mkB[mi]]
                    return f
                tiles = [dict(kt=KT[:, 0, g, j * 128:(j + 1) * 128], kR=[KTB], va=Vt[:, j, 0, g, 0:129], vR=[VtB],
                              nk=128, mask=selmask(j)) for j in list(range(8, 16)) + list(range(0, s + 1))]
                attend(4, 128, qrhs, qR, tiles, [0, 1], flush=False)
                after.append(lambda on=on, onB_=onB_, bg3=bg3, g=g: fin_branch([0, 1], on, onB_, lambda hp: bg3[:, 4 * g + 2 * hp:4 * g + 2 * hp + 2, 1]))
                tiles = []
                for t5, j in enumerate(win_tiles(s)):
                    interior = (j < 8) and (s - 3 <= j <= s - 1)
                    tiles.append(dict(kt=KT[:, 1, g, j * 128:(j + 1) * 128], kR=[KTB], va=Vt[:, j, 1, g, 0:129], vR=[VtB],
                                      nk=128, mask=None if interior else (winm[:, s, t5, :], [winmB])))
                attend(4, 128, qrhs, qR, tiles, [2, 3], flush=False)

                def fin_unit(on=on, onB_=onB_, bg3=bg3, g=g, s=s):
                    fin_branch([2, 3], on, onB_, lambda hp: bg3[:, 4 * g + 2 * hp:4 * g + 2 * hp + 2, 2])
                    act(onb[:], on[:], AF.Copy, onB_, [onbB])
                    for h in range(4):
                        tr(psb16(6)[:, h * 128:(h + 1) * 128], onb[:, h, :], identb[:], [onbB, identbB], [psB[6]])
                    tt(mixedT[:, 4 + 4 * g:8 + 4 * g, s * 128:(s + 1) * 128],
                       psb16(6)[:, 0:512].rearrange("p (h q) -> p h q", h=4),
                       arB[:, 4 * g:4 * g + 4, s * 128:(s + 1) * 128], ALU.mult,
                       [psB[6]] + [arBB[4 * g + h] for h in range(4)], [mixB])
                after.append(fin_unit)
        att_flush()
        p1_on[0] = False
        while p1["p"] < 128:
            pass1_step()
        plf = pl[:].rearrange("p g s c -> p g (s c)")
        for g in range(2):
            mm(ps[6][:, 0:256], wc2[:, 0, :], plf[:, g, :], True, True, [wc2B, plB], [psB[6]])
            cp(kcTs[:, g, :], ps[6][:, 0:256], [psB[6]], [kcsB])
            for nt_ in range(2):
                mm(ps[7][:, 0:128], plf[:, 2 + g, nt_ * 128:(nt_ + 1) * 128], wc2[:, 1, :], True, True, [wc2B, plB], [psB[7]])
                cp(vcs[:, nt_, g, :], ps[7][:, 0:128], [psB[7]], [kcsB])
        S.barrier()
        e2.close()
        esB.close()
        if STOP[0] == 2:
            S.final_wait("sp")
            esA.close()
            return nc

        e3 = ExitStack(); e3.__enter__(); cur[0] = e3
        wt = [sb("wt3_%d" % i, [128, 16, 128], BF16) for i in range(2)]; wtB = [Buf() for _ in range(2)]
        PT = [sb("PT0b", [128, 512], BF16), sb("PT1b", [128, 512], BF16)]; PTB = [Buf(), Buf()]
        om = sb("om", [128, 4, 128], BF16); omB = Buf()
        arM = sb("arM", [128, 4, 1024], BF16); arMB = [Buf() for _ in range(4)]
        rv = sb("rvb", [128, 8]); rvB = Buf()
        cmb = sb("cmb", [128, 2, 1024], BF16); cmbB = Buf()
        memKTs = sb("memKTs", [128, 4, 256], BF16); memKsB = Buf()
        memVs = sb("memVs", [128, 2, 4, 130], BF16); memVsB = Buf()
        oms = sb("oms", [4, 4, 128], BF16); omsB = Buf()
        S.dma("pool", S.newsem("c_cmb"), cmb[:], cmem.rearrange("(t p) c -> p t c", p=128), writes=[cmbB])
        S.op("dve", lambda e: e.memset(memVs[:, :, :, 128:130], 1.0), writes=[memVsB])
        bankset = [0]
        for bi, (kind, a0, a1, col) in enumerate(blocks):
            if kind not in ("mq", "mg"):
                continue
            ws = bi % 2
            S.dma("pool", wtsem[ws], wt[ws][:], win_t[bi], writes=[wtB[ws]])
            bs = bankset[0]
            bankset[0] = 3 - bs
            b0, b1, bx = bs, bs + 1, bs + 2
            for k in range(16):
                for gi, bank in ((0, b0), (1, b1)):
                    mm(ps[bank][:, :], wt[ws][:, k, :], hT_own[:, k, gi * 512:(gi + 1) * 512], k == 0, k == 15,
                       [wtB[ws], hT_ownB], [psB[bank]])
                mm(ps[bx][:, 0:NX], wt[ws][:, k, :], hTx[:, k, :], k == 0, k == 15, [wtB[ws], hTxB], [psB[bx]])
            fn_ = AF.Copy if kind == "mq" else AF.Silu
            dst, dB_ = (QT, QTB[a0]) if kind == "mq" else (arM, arMB[a0])
            act(dst[:, a0, 0:512], ps[b0][:, :], fn_, [psB[b0]], [dB_])
            act(dst[:, a0, 512:1024], ps[b1][:, :], fn_, [psB[b1]], [dB_])
            act(arBs[:, a0 + (8 if kind == "mq" else 12), :], ps[bx][:, 2:6], fn_, [psB[bx]], [arBsB])
        for qg in range(2):
            for hm in range(4):
                tiles = [dict(kt=memKT[:, hm, mt * 128:(mt + 1) * 128], kR=[memKB], va=memV[:, mt, hm, 0:129], vR=[memVB],
                              nk=128, mask=None) for mt in range(2)]
                attend(4, 128, QT[:, hm, qg * 512:(qg + 1) * 512], [QTB[hm]], tiles, [0, 1])
                for hp in range(2):
                    o3 = ps[hp][:, 0:260].rearrange("p (h c) -> p h c", h=2)
                    S.op("dve", lambda e: e.reciprocal(out=rv[:, 0:2], in_=o3[:, :, 128]), reads=[psB[hp]], writes=[rvB])
                    for hh in range(2):
                        tsc(om[:, 2 * hp + hh, :], o3[:, hh, 0:128], rv[:, hh:hh + 1], None, ALU.mult, None,
                            [psB[hp], rvB], [omB])
                for qt in range(4):
                    tr(psb16(6)[:, qt * 128:(qt + 1) * 128], om[:, qt, :], identb[:], [omB, identbB], [psB[6]])
                tt(mixedT[:, 12 + hm, qg * 512:(qg + 1) * 512], psb16(6)[:, 0:512], arM[:, hm, qg * 512:(qg + 1) * 512],
                   ALU.mult, [psB[6], arMB[hm]], [mixB])
        for mt in range(2):
            for hm in range(4):
                tr(psb16(6)[:, hm * 128:(hm + 1) * 128], cmb[:, mt, hm * 128:(hm + 1) * 128], identb[:], [cmbB, identbB], [psB[6]])
            cp(memKTs[:, :, mt * 128:(mt + 1) * 128], psb16(6)[:, 0:512].rearrange("p (h t) -> p h t", h=4), [psB[6]], [memKsB])
            cp(memVs[:, mt, :, 0:128], cmb[:, mt, 512:1024].rearrange("p (h d) -> p h d", h=4), [cmbB], [memVsB])
        for hm in range(4):
            tiles = [dict(kt=memKTs[:, hm, mt * 128:(mt + 1) * 128], kR=[memKsB], va=memVs[:, mt, hm, 0:129], vR=[memVsB],
                          nk=128, mask=None) for mt in range(2)]
            attend(1, 4, arBs[:, 8 + hm, :], [arBsB], tiles, [0])
            S.op("dve", lambda e: e.reciprocal(out=rv[0:4, 0:1], in_=ps[0][0:4, 128:129]), reads=[psB[0]], writes=[rvB])
            tsc(oms[0:4, hm, :], ps[0][0:4, 0:128], rv[0:4, 0:1], None, ALU.mult, None, [psB[0], rvB], [omsB])
        for hm in range(4):
            tr(psb16(6)[:, hm * 4:(hm + 1) * 4], oms[0:4, hm, :], identb[0:4, 0:4], [omsB, identbB], [psB[6]])
        tt(mixs[:, 12:16, :], psb16(6)[:, 0:16].rearrange("p (h t) -> p h t", h=4), arBs[:, 12:16, :], ALU.mult,
           [psB[6], arBsB], [mixsB])
        S.barrier()
        e3.close()
        esA.close()
        if STOP[0] == 3:
            S.final_wait("sp")
            return nc


        e5 = ExitStack(); e5.__enter__(); cur[0] = e5
        wo = [sb("wo0", [128, 16, 1024], BF16), sb("wo1", [128, 16, 1024], BF16)]; woB = [Buf(), Buf()]
        for hf in range(2):
            S.dma("pool", wtsem[hf], wo[hf][:], wo_t[hf], writes=[woB[hf]])
        PT = [sb("PT0s", [128, 512], BF16), sb("PT1s", [128, 512], BF16)]; PTB = [Buf(), Buf()]
        rv = sb("rvs", [128, 8]); rvB = Buf()
        rsel, rselB = const("rsel", rseld, [16, 4])
        ptcol, ptcolB = const("ptcol", ptcold, [128, 1], I32)
        colB, colBB = const("colB", colBd, [128, 128])
        idxf = sb("idxf", [128, 128]); idxfB = Buf()
        idxB = sb("idxB", [128, 128], I32); idxBB = Buf()
        pg1 = sb("pg1", [128, 2]); pg1B = Buf()
        cp(pg1[:, 0:1], ptcol[:], [ptcolB], [pg1B])
        tsc(pg1[:, 1:2], pg1[:, 0:1], 256.0, None, ALU.mult, None, [pg1B], [pg1B])
        tsc(idxf[:], colB[:], pg1[:, 1:2], None, ALU.add, None, [idxfB, colBB, pg1B], [idxfB])
        cp(idxB[:], idxf[:], [idxfB], [idxBB])

        rselT, rselTB = const("rselT", rselTd, [4, 16])
        oneh, onehB = const("oneh", onehd, [16, 6, 24])
        maskw16, maskw16B = const("maskw16", maskw16d, [128, 4, 16], BF16, "pool")
        maskn16, maskn16B = const("maskn16", maskn16d, [4, 16], BF16, "pool")
        G16 = sb("G16", [16, 8]); G16B = Buf()
        tG = sb("tG", [16, 6, 24]); tGB = Buf()
        mm(ps[6][0:16, 0:24], rselT[:, :], bgTs[0:4, :], True, True, [rselTB, bgTsB], [psB[6]])
        tt(tG[:], oneh[:], ps[6][0:16, 0:24].unsqueeze(1).to_broadcast([16, 6, 24]), ALU.mult, [onehB, psB[6]], [tGB])
        S.op("dve", lambda e: e.tensor_reduce(out=G16[:, 0:6], in_=tG[:], axis=AX.X, op=ALU.add), reads=[tGB], writes=[G16B])

        e16 = sb("e16", [16, 256]); e16B = Buf()
        sms = sb("sms", [16, 4]); smsB = Buf()
        pTs = sb("pTs", [128, 2, 16], BF16); pTsB = Buf()
        impS = sb("impS", [4, 3, 256]); impSB = Buf()
        m8s = sb("m8s", [4, 16]); m8sB = Buf()
        on16 = [sb("on16_0", [16, 128]), sb("on16_1", [16, 128])]; on16B = [Buf(), Buf()]
        maskB2 = sb("maskB2", [128, 2, 2, 4], BF16); maskB2B = Buf()
        maskB16 = sb("maskB16", [128, 2, 2, 16], BF16); maskB16B = Buf()
        e16b = sb("e16b", [16, 256], BF16); e16bB = Buf()
        qs16 = [qsT[:, 4 * g:4 * g + 4, :].rearrange("p h t -> p (h t)") for g in range(2)]
        qr16 = [qrs[:, 4 * g:4 * g + 4, :].rearrange("p h t -> p (h t)") for g in range(2)]
        for g in range(2):
            mm(ps[6][0:16, 0:256], qs16[g], kcTs[:, g, :], True, True, [qsB, kcsB], [psB[6]])
            act(e16[:], ps[6][0:16, 0:256], AF.Exp, [psB[6]], [e16B], scale=SCALE)
            S.op("dve", lambda e: e.tensor_reduce(out=sms[:, 0:1], in_=e16[:], axis=AX.X, op=ALU.add), reads=[e16B], writes=[smsB])
            S.op("dve", lambda e: e.reciprocal(out=sms[:, 1:2], in_=sms[:, 0:1]), reads=[smsB], writes=[smsB])
            tsc(e16[:], e16[:], sms[:, 1:2], None, ALU.mult, None, [e16B, smsB], [e16B])
            cp(e16b[:], e16[:], [e16B], [e16bB])
            for nt_ in range(2):
                tr(psb16(7)[:, nt_ * 16:(nt_ + 1) * 16], e16b[:, nt_ * 128:(nt_ + 1) * 128], identb[0:16, 0:16],
                   [e16bB, identbB], [psB[7]])
            cp(pTs[:], psb16(7)[:, 0:32].rearrange("p (n q) -> p n q", n=2), [psB[7]], [pTsB])
            for nt_ in range(2):
                mm(ps[7][0:16, 0:128], pTs[:, nt_, :], vcs[:, nt_, g, :], nt_ == 0, nt_ == 1, [pTsB, kcsB], [psB[7]])
            tsc(on16[g][:], ps[7][0:16, 0:128], G16[:, 3 * g:3 * g + 1], None, ALU.mult, None, [psB[7], G16B], [on16B[g]])
            mm(ps[6][0:4, 256:512], rsel[:, :], e16[:], True, True, [rselB, e16B], [psB[6]])
            cp(impS[:, 0, :], ps[6][0:4, 256:512], [psB[6]], [impSB])
            S.op("dve", lambda e: e.memset(impS[:, 0, 0:1], -1.0), reads=[impSB], writes=[impSB])
            S.op("dve", lambda e: e.memset(impS[:, 0, 255:256], -1.0), reads=[impSB], writes=[impSB])
            S.op("dve", lambda e: e.max(out=m8s[:, 0:8], in_=impS[:, 0, :]), reads=[impSB], writes=[m8sB])
            S.op("dve", lambda e: e.match_replace(out=impS[:, 1, :], in_to_replace=m8s[:, 0:8], in_values=impS[:, 0, :],
                                                  imm_value=-2.0), reads=[impSB, m8sB], writes=[impSB])
            S.op("dve", lambda e: e.max(out=m8s[:, 8:16], in_=impS[:, 1, :]), reads=[impSB], writes=[m8sB])
            tsc(impS[:, 2, :], impS[:, 0, :], m8s[:, 12:13], None, ALU.is_ge, None, [impSB, m8sB], [impSB])
            S.op("dve", lambda e: e.memset(impS[:, 2, 0:1], 1.0), reads=[impSB], writes=[impSB])
            S.op("dve", lambda e: e.memset(impS[:, 2, 255:256], 1.0), reads=[impSB], writes=[impSB])
            sel3 = impS[:, 2, :].rearrange("p (s c) -> p s c", c=2)
            for c in range(2):
                tr(ps[6][:, c * 4:(c + 1) * 4], sel3[:, :, c], identf[0:4, 0:4], [impSB, identfB], [psB[6]])
            cp(maskB2[:, :, g, :], ps[6][:, 0:8].rearrange("p (c t) -> p c t", c=2), [psB[6]], [maskB2B])
            cp(maskB16[:, :, g, :].rearrange("p c (h t) -> p c h t", h=4),
               maskB2[:, :, g, :].unsqueeze(2).to_broadcast([128, 2, 4, 4]), [maskB2B], [maskB16B])

        knew = sb("knew", [128, 2, 2, 4], BF16); knewB = Buf()
        vnew = sb("vnew", [4, 2, 2, 130], BF16); vnewB = Buf()
        S.op("dve", lambda e: e.memset(vnew[:, :, :, 128:130], 1.0), writes=[vnewB])
        for br in range(2):
            for g in range(2):
                cp(knew[:, br, g, :], kvsT[:, 4 + 4 * br + g, :], [kvsB], [knewB])
                tr(ps[7][0:4, 0:128], kvsT[:, 6 + 4 * br + g, :], identf[:, :], [kvsB, identfB], [psB[7]])
                cp(vnew[:, br, g, 0:128], ps[7][0:4, 0:128], [psB[7]], [vnewB])

        def fin16(bank, g, br, off=0):
            ri = n_rv[0] % 4
            n_rv[0] += 1
            R_, RB_ = rvr[ri], rvrB[ri]
            S.op("dve", lambda e: e.reciprocal(out=R_[0:16, 0:1], in_=ps[bank][0:16, off + 128:off + 129]),
                 reads=[psB[bank]], writes=[RB_])
            tt(R_[0:16, 1:2], R_[0:16, 0:1], G16[:, 3 * g + br:3 * g + br + 1], ALU.mult, [RB_, G16B], [RB_])
            stt(on16[g][:], ps[bank][0:16, off:off + 128], R_[0:16, 1:2], on16[g][:], ALU.mult, ALU.add,
                [psB[bank], RB_, on16B[g]], [on16B[g]])

        NSL = 8
        stl = [sb("stl%d" % i, [128, 512], BF16) for i in range(NSL)]; stlB = [Buf() for _ in range(NSL)]
        stsem2 = [S.newsem("stl%d" % i) for i in range(NSL)]
        kts = [sb("kts%d" % i, [128, 512], BF16) for i in range(2)]; ktsB = [Buf(), Buf()]
        vts = [sb("vts%d" % i, [128, 4, 2, 130], BF16) for i in range(2)]; vtsB = [Buf(), Buf()]
        for i in range(2):
            S.op("dve", lambda e: e.memset(vts[i][:, :, :, 128:130], 1.0), writes=[vtsB[i]])
        obank = [0, 2]
        pend2 = []

        def stepB(i, vi, g, r4):
            for rr in range(4):
                mm(ps[4][0:16, g * 130:g * 130 + 129], PT[i][:, rr * 16:(rr + 1) * 16], vts[vi][:, rr, g, 0:129],
                   r4 == 0 and rr == 0 and g == 0, False, [PTB[i], vtsB[vi]], [psB[4]], skip=True)

        def pass2_issue(r4):
            for rr in range(4):
                r = r4 * 4 + rr
                sl = r % NSL
                S.dma("pool", stsem2[sl], stl[sl][:], cache2, reads=[idxBB], writes=[stlB[sl]], indirect=idxB[:, r:r + 1])

        def pass2_step(r4):
            vi = r4 % 2
            for rr in range(4):
                sl = (r4 * 4 + rr) % NSL
                act(vts[vi][:, rr, :, 0:128], stl[sl][:, 256:512].rearrange("p (g d) -> p g d", g=2), AF.Copy,
                    [stlB[sl]], [vtsB[vi]])
            for g in range(2):
                ki = (r4 * 2 + g) % 2
                i = n_att[0] % 2
                n_att[0] += 1
                for rr in range(4):
                    sl = (r4 * 4 + rr) % NSL
                    tr(psb16(6 + ki)[:, rr * 128:(rr + 1) * 128], stl[sl][:, g * 128:(g + 1) * 128], identb[:],
                       [stlB[sl], identbB], [psB[6 + ki]])
                cp(kts[ki][:], psb16(6 + ki)[:, 0:512], [psB[6 + ki]], [ktsB[ki]])
                for rr in range(4):
                    mm(ps[5][:, rr * 16:(rr + 1) * 16], kts[ki][:, rr * 128:(rr + 1) * 128], qr16[g], True, True,
                       [ktsB[ki], qrsB], [psB[5]])
                while pend2:
                    stepB(*pend2.pop(0))
                act(PT[i][:, 0:64], ps[5][:, 0:64], AF.Exp, [psB[5]], [PTB[i]], scale=SCALE)
                p3 = PT[i][:, 0:64].rearrange("p (r q) -> p r q", r=4)
                tt(p3, p3, maskB16[:, (r4 * 4) // 64, g, :].unsqueeze(1).to_broadcast([128, 4, 16]), ALU.mult,
                   [PTB[i], maskB16B], [PTB[i]])
                pend2.append((i, vi, g, r4))

        wtl = sb("wtl", [128, 4, 512], BF16); wtlB = Buf()
        S.dma("pool", S.newsem("c_wtl"), wtl[:], cwin.rearrange("(t p) c -> p t c", p=128), writes=[wtlB])
        vtw = sb("vtw", [128, 4, 2, 130], BF16); vtwB = Buf()
        S.op("dve", lambda e: e.memset(vtw[:, :, :, 128:130], 1.0), writes=[vtwB])
        for t4 in range(4):
            cp(vtw[:, t4, :, 0:128], wtl[:, t4, 256:512].rearrange("p (g d) -> p g d", g=2), [wtlB], [vtwB])
        for g in range(2):
            ki = g % 2
            for t4 in range(4):
                tr(psb16(6 + ki)[:, t4 * 128:(t4 + 1) * 128], wtl[:, t4, g * 128:(g + 1) * 128], identb[:],
                   [wtlB, identbB], [psB[6 + ki]])
            cp(kts[ki][:], psb16(6 + ki)[:, 0:512], [psB[6 + ki]], [ktsB[ki]])
            tiles = [dict(kt=kts[ki][:, t4 * 128:(t4 + 1) * 128], kR=[ktsB[ki]], va=vtw[:, t4, g, 0:129], vR=[vtwB], nk=128,
                          mask=(maskw16[:, t4, :], [maskw16B])) for t4 in range(4)]
            tiles.append(dict(kt=knew[:, 1, g, :], kR=[knewB], va=vnew[:, 1, g, 0:129], vR=[vnewB], nk=4,
                              mask=(maskn16[:, :], [maskn16B])))
            attend(1, 16, qr16[g], [qrsB], tiles, [obank[g] + 1])
            fin16(obank[g] + 1, g, 2)
        gbc, gbcB = const("gbc4", gf, [128, D])
        junk = sb("junk4", [128, D], BF16)
        xs = [sb("xs0b", [128, D]), sb("xs1b", [128, D])]; xsB = [Buf(), Buf()]
        yb = [sb("yb0", [128, D]), sb("yb1", [128, D])]; ybB = [Buf(), Buf()]

        def outproj_slot(s, hook=None):
            j = s % 2
            npart = 128 if s < 8 else 4
            src = xall[s * 128:(s + 1) * 128, :] if s < 8 else xext[2:6, :]
            mR = [mixB] if s < 8 else [mixsB]
            S.dma("sp", xsem[j], xs[j][:npart, :], src, writes=[xsB[j]])
            for cgp in range(4):
                for kc in range(16):
                    lhsT = mixedT[:, kc, s * 128:(s + 1) * 128] if s < 8 else mixs[:, kc, :]
                    mm(ps[cgp][0:npart, :], lhsT, wo[cgp // 2][:, kc, (cgp % 2) * 512:(cgp % 2 + 1) * 512], kc == 0, kc == 15,
                       mR + [woB[cgp // 2]], [psB[cgp]])
                tt(yb[j][:npart, cgp * 512:(cgp + 1) * 512], ps[cgp][0:npart, :], xs[j][:npart, cgp * 512:(cgp + 1) * 512],
                   ALU.add, [psB[cgp], xsB[j]], [ybB[j]])
                if hook is not None:
                    hook()
            norm_stats(20 + s, yb[j][:npart, :], ybB[j], npart)
            stt(yb[j][:npart, :], yb[j][:npart, :], st[:npart, 20 + s, 3:4], gbc[:npart, :], ALU.mult, ALU.mult,
                [ybB[j], stB[20 + s], gbcB], [ybB[j]])
            dst = y_own[s * 128:(s + 1) * 128, :] if s < 8 else y_s[:, :]
            S.dma("sp", stsem[j], dst, yb[j][:npart, :], reads=[ybB[j]])

        p2n = [0]

        pass2_issue(0)

        def hook():
            if p2n[0] < 32:
                if p2n[0] + 1 < 32:
                    pass2_issue(p2n[0] + 1)
                pass2_step(p2n[0])
                p2n[0] += 1

        for s in range(8):
            outproj_slot(s, hook)
        while p2n[0] < 32:
            hook()
        while pend2:
            stepB(*pend2.pop(0))
        sbase[0] = 6
        for g in range(2):
            t = dict(kt=knew[:, 0, g, :], kR=[knewB], va=vnew[:, 0, g, 0:129], vR=[vnewB], nk=4, mask=(maskn16[:, :], [maskn16B]),
                     ooff=g * 130)
            att_tile(1, 16, qr16[g], [qrsB], t, [4], False, True)
            att_flush()
            fin16(4, g, 1, off=g * 130)
        onb16 = sb("onb16", [16, 128], BF16); onb16B = Buf()
        for g in range(2):
            cp(onb16[:], on16[g][:], [on16B[g]], [onb16B])
            tr(psb16(6)[:, 0:16], onb16[:, :], identb[0:16, 0:16], [onb16B, identbB], [psB[6]])
            tt(mixs[:, 4 + 4 * g:8 + 4 * g, :], psb16(6)[:, 0:16].rearrange("p (h t) -> p h t", h=4), arBs[:, 4 * g:4 * g + 4, :],
               ALU.mult, [psB[6], arBsB], [mixsB])
        outproj_slot(8)
        S.final_wait("sp")
        e5.close()
    return nc


_PROG = {}


def rope_tables(pos):
    half = 16
    freqs = np.power(np.float32(500000.0), -np.arange(half, dtype=np.float32) * np.float32(2.0 / 32)).astype(np.float32)
    ang = pos.astype(np.float32)[None, :] * freqs[:, None]
    c = np.cos(ang).astype(np.float32)
    s = np.sin(ang).astype(np.float32)
    return np.concatenate([c, c], 0), np.concatenate([-s, s], 0)


def kernel(x_prompt, x_sample, cache_kv, cache_win, state_conv, cache_mem, page_table, mem_prompt,
           norm_g, w_in, w_conv, a_cmp, w_cmp, mem_norm_g, w_mem_kv, w_out, final_g):
    f32 = np.float32
    x_prompt = np.asarray(x_prompt, f32); x_sample = np.asarray(x_sample, f32)
    w_in = np.asarray(w_in, f32)
    n_pool = cache_kv.shape[1]
    if "nc" not in _PROG:
        _PROG["nc"] = build_program(n_pool)
    nc = _PROG["nc"]

    blocks = block_list()
    cols = np.stack([(c + (np.arange(128) % 24 if k == "bg" else np.arange(128))) for (k, _, _, c) in blocks])
    w3 = w_in[0].reshape(16, 128, -1)
    win_t = np.ascontiguousarray(w3[:, :, cols].transpose(2, 1, 0, 3))
    wm_t = np.ascontiguousarray(np.asarray(w_mem_kv, f32)[0].reshape(16, 128, 4, 256).transpose(2, 1, 0, 3))
    wo_t = np.ascontiguousarray(np.asarray(w_out, f32)[0].reshape(16, 128, 2, 1024).transpose(2, 1, 0, 3))
    rep = lambda v: np.ascontiguousarray(np.broadcast_to(np.asarray(v, f32).reshape(1, -1), (128, D)))
    gn, gm, gf = rep(norm_g[0]), rep(mem_norm_g[0]), rep(final_g)
    identd = np.eye(128, dtype=f32)
    pmd = np.zeros((32, 32), f32)
    for m in range(32):
        pmd[(m + 16) % 32, m] = 1.0
    aTd = np.ascontiguousarray(np.asarray(a_cmp, f32)[0].transpose(2, 0, 1))
    wcd = np.ascontiguousarray(np.asarray(w_cmp, f32)[0].transpose(1, 0, 2))
    wcvd = np.ascontiguousarray(np.asarray(w_conv, f32)[0].reshape(3, 4, 128).transpose(2, 1, 0))
    trid = (np.arange(128)[:, None] <= np.arange(128)[None, :]).astype(f32)

    cache2 = np.ascontiguousarray(np.asarray(cache_kv, f32)[0]).reshape(n_pool * 256, 512)
    rowAd = (2.0 * np.arange(128, dtype=f32)).reshape(128, 1)
    colBd = np.ascontiguousarray(np.broadcast_to((2.0 * np.arange(128, dtype=f32) + 1.0)[None, :], (128, 128)))
    a0 = np.asarray(a_cmp, f32)[0]
    arepd = np.ascontiguousarray(np.broadcast_to(a0[:, np.arange(128) % 64, None, :], (2, 128, 2, 128)).transpose(1, 0, 2, 3)
                                 ).reshape(128, 512)
    ind2d = (np.arange(128)[:, None] // 64 == np.arange(2)[None, :]).astype(f32)
    rseld = (np.arange(16)[:, None] % 4 == np.arange(4)[None, :]).astype(f32)
    maskwd = ((np.arange(4)[None, :, None] * 128 + np.arange(128)[:, None, None]) > np.arange(4)[None, None, :]).astype(f32)
    masknd = (np.arange(4)[:, None] <= np.arange(4)[None, :]).astype(f32)
    maskw16d = np.ascontiguousarray(maskwd[:, :, np.arange(16) % 4])
    maskn16d = np.ascontiguousarray(masknd[:, np.arange(16) % 4])
    rselTd = np.ascontiguousarray(rseld.T)
    onehd = np.zeros((16, 6, 24), f32)
    for hh in range(4):
        for tt_ in range(4):
            for gg in range(2):
                for bb in range(3):
                    onehd[hh * 4 + tt_, gg * 3 + bb, (4 * gg + hh) * 3 + bb] = 1.0
    in_maps = []
    for c in range(8):
        b, par, i = c // 2, c % 2, c
        own = slice(par * 1024, (par + 1) * 1024)
        oth = slice((1 - par) * 1024, (2 - par) * 1024)
        xall = np.concatenate([x_prompt[b, own], x_prompt[b, oth]], 0)
        xext = np.zeros((NX, D), f32)
        if par == 1:
            xext[0:2] = x_prompt[b, 1022:1024]
        xext[2:6] = x_sample[i]
        pos = np.concatenate([np.arange(par * 1024, (par + 1) * 1024), np.arange((1 - par) * 1024, (2 - par) * 1024)])
        posx = np.concatenate([np.zeros(2), 16384 + np.arange(4), np.zeros(2)])
        cosT, sinT = rope_tables(np.concatenate([pos[:1024], posx, pos[1024:]]))
        winmd = np.zeros((128, 8, 5, 128), f32)
        cmpcd = np.zeros((128, 8, 3, 32), f32)
        blkabs = pos[::64] // 64
        for s in range(8):
            qpos = pos[s * 128:(s + 1) * 128]
            for t5, j in enumerate(win_tiles(s)):
                kpos = pos[j * 128:(j + 1) * 128]
                dlt = qpos[None, :] - kpos[:, None]
                winmd[:, s, t5, :] = ((dlt >= 0) & (dlt < 512)).astype(f32)
            valid = ((blkabs[None, :] + 1) * 64 <= qpos[:, None] + 1)
            cur = (qpos // 64)[:, None]
            forced = (blkabs[None, :] == 0) | (blkabs[None, :] == cur) | (blkabs[None, :] == cur - 1)
            fut = blkabs[None, :] > cur
            cmpcd[:, s, 0, :] = np.where(valid, 0.0, -1e5)
            cmpcd[:, s, 1, :] = (~forced & ~fut).astype(f32)
            cmpcd[:, s, 2, :] = np.where(fut, -1.0, np.where(forced, 1e9, 0.0))
        ealld = np.zeros((32, 16, 128), f32)
        for j in range(16):
            for kk in range(128):
                ealld[2 * j + kk // 64, j, kk] = 1.0 if j < 8 else float(par)
        stcd = np.ascontiguousarray(np.asarray(state_conv, f32)[0, i].reshape(2, 4, 128).transpose(2, 1, 0))
        pti = np.asarray(page_table)[i].astype(np.int32)
        in_maps.append(dict(
            xall=xall, xext=xext, gn=gn, gm=gm, gf=gf, win_t=win_t, wm_t=wm_t, wo_t=wo_t,
            memx=np.ascontiguousarray(np.asarray(mem_prompt, f32)[b]), cosT=cosT, sinT=sinT, identd=identd, pmd=pmd,
            aTd=aTd, wcd=wcd, wcvd=wcvd, stcd=stcd, trid=trid, winmd=winmd, cmpcd=cmpcd, ealld=ealld,
            cwin=np.ascontiguousarray(np.asarray(cache_win, f32)[0, i].reshape(512, 512)),
            cmem=np.ascontiguousarray(np.asarray(cache_mem, f32)[0, i].reshape(256, 1024)),
            cache2=cache2, ptrepd=np.ascontiguousarray(np.broadcast_to(pti[None, :], (128, 128))),
            ptcold=np.ascontiguousarray(pti.reshape(128, 1)), rowAd=rowAd, colBd=colBd, arepd=arepd, ind2d=ind2d, rseld=rseld,
            maskw16d=maskw16d, maskn16d=maskn16d, rselTd=rselTd, onehd=onehd,
        ))
    res = run_bass_kernel_spmd(nc, in_maps, core_ids=list(range(8)))
    R = res.results
    y_prompt = np.zeros((4, 2048, D), f32); kv_p = np.zeros((1, 4, 2048, 4, 2, 128), f32)
    for c in range(8):
        b, par = c // 2, c % 2
        y_prompt[b, par * 1024:(par + 1) * 1024] = R[c]["y_own"]
        kv_p[0, b, par * 1024:(par + 1) * 1024] = R[c]["kv_own"].reshape(1024, 4, 2, 128)
    y_sample = np.stack([R[c]["y_s"] for c in range(8)]).reshape(8, 4, D)
    win_p = np.stack([R[2 * b + 1]["win_own"] for b in range(4)]).reshape(1, 4, 512, 2, 2, 128)
    conv_pp = np.stack([R[2 * b + 1]["conv_p"] for b in range(4)]).reshape(1, 4, 2, 512)
    mem_p = np.stack([R[2 * b]["memkv"] for b in range(4)]).reshape(1, 4, 256, 2, 4, 128)
    kv_ss = np.stack([R[c]["kv_s"] for c in range(8)]).reshape(1, 8, 4, 4, 2, 128)
    win_ss = np.stack([R[c]["win_s"] for c in range(8)]).reshape(1, 8, 512, 2, 2, 128)
    conv_ss = np.stack([R[c]["conv_s"] for c in range(8)]).reshape(1, 8, 2, 512)
    return (y_prompt, y_sample, kv_p, win_p, conv_pp, mem_p, kv_ss, win_ss, conv_ss)


def win_tiles(s):
    own = [j for j in range(s - 4, s + 1) if j >= 0]
    oth = [8 + j for j in range(s + 4, 8)]
    return oth + own
```

```python
import numpy as np
from contextlib import ExitStack
import concourse.bass as bass
import concourse.mybir as mybir
from concourse.bass_utils import run_bass_kernel_spmd

F32 = mybir.dt.float32
BF16 = mybir.dt.bfloat16
I32 = mybir.dt.int32
ALU = mybir.AluOpType
AF = mybir.ActivationFunctionType
AX = mybir.AxisListType

D = 2048
NX = 8
SCALE = 128 ** -0.5
EPS = 1e-6
NBLK = 53
O_CH, O_CB, O_CC, O_CG, O_Q, O_NG, O_BG, O_KV, O_MQ, O_MG = 0, 512, 1024, 1536, 2048, 3072, 4096, 4120, 5656, 6168


def block_list():
    bl = []
    for part in range(6):
        for g in range(2):
            bl.append(("kv", part, g, O_KV + part * 256 + g * 128))
    for cb in range(4):
        bl.append(("ch", cb, 0, O_CH + cb * 128))
        bl.append(("cc", cb, 0, O_CC + cb * 128))
        bl.append(("cb", cb, 0, O_CB + cb * 128))
        bl.append(("cg", cb, 0, O_CG + cb * 128))
    for h in range(8):
        bl.append(("q", h, 0, O_Q + h * 128))
    for h in range(8):
        bl.append(("ng", h, 0, O_NG + h * 128))
    bl.append(("bg", 0, 0, O_BG))
    for h in range(4):
        bl.append(("mq", h, 0, O_MQ + h * 128))
    for h in range(4):
        bl.append(("mg", h, 0, O_MG + h * 128))
    return bl


class Buf:
    __slots__ = ("w", "r", "x")

    def __init__(self, x=False):
        self.w = None
        self.r = {}
        self.x = x


class Sched:
    def __init__(self, nc, es):
        self.nc, self.es = nc, es
        self.eng = {"pe": nc.tensor, "act": nc.scalar, "dve": nc.vector, "pool": nc.gpsimd, "sp": nc.sync}
        self.semobj, self.sem, self.cnt, self.dmacnt = {}, {}, {}, {}
        self.waited = {e: {} for e in self.eng}
        for e in self.eng:
            self.sem[e] = self._reg(es.enter_context(nc.semaphore("q_" + e)))
            self.cnt[e] = 0

    def _reg(self, h):
        i = len(self.semobj)
        self.semobj[i] = h
        return i

    def newsem(self, name):
        i = self._reg(self.es.enter_context(self.nc.semaphore(name)))
        self.dmacnt[i] = 0
        return i

    def _deps(self, eng, reads, writes):
        d = {}

        def add(k, v):
            if eng == "pe" and k == self.sem["pe"]:
                return
            if d.get(k, 0) < v:
                d[k] = v

        for b in reads:
            if b.w is not None:
                add(*b.w)
            if b.x:
                for k, v in b.r.items():
                    if k != self.sem.get(eng):
                        add(k, v)
        for b in writes:
            if b.w is not None:
                add(*b.w)
            for k, v in b.r.items():
                add(k, v)
        w = self.waited[eng]
        for k, v in d.items():
            if w.get(k, 0) < v:
                self.eng[eng].wait_ge(self.semobj[k], v)
                w[k] = v

    def _mark(self, tok, reads, writes):
        for b in reads:
            if b.r.get(tok[0], 0) < tok[1]:
                b.r[tok[0]] = tok[1]
        for b in writes:
            b.w = tok
            b.r = {}

    def op(self, eng, fn, reads=(), writes=()):
        self._deps(eng, reads, writes)
        ins = fn(self.eng[eng])
        self.cnt[eng] += 1
        ins.then_inc(self.semobj[self.sem[eng]], 1)
        self._mark((self.sem[eng], self.cnt[eng]), reads, writes)

    def dma(self, q, sem, out, in_, reads=(), writes=(), indirect=None, **kw):
        self._deps(q, reads, writes)
        if indirect is None:
            ins = self.eng[q].dma_start(out=out, in_=in_, **kw)
        else:
            ins = self.eng[q].indirect_dma_start(out=out, out_offset=None, in_=in_,
                                                 in_offset=bass.IndirectOffsetOnAxis(ap=indirect, axis=0))
        self.dmacnt[sem] += 16
        ins.then_inc(self.semobj[sem], 16)
        self._mark((sem, self.dmacnt[sem]), reads, writes)

    def barrier(self):
        for e in self.eng:
            w = self.waited[e]
            for e2 in self.eng:
                k2, v2 = self.sem[e2], self.cnt[e2]
                if e2 != e and v2 > w.get(k2, 0):
                    self.eng[e].wait_ge(self.semobj[k2], v2)
                    w[k2] = v2
            for k2, v2 in self.dmacnt.items():
                if v2 > w.get(k2, 0):
                    self.eng[e].wait_ge(self.semobj[k2], v2)
                    w[k2] = v2

    def final_wait(self, eng="sp"):
        for k, v in self.dmacnt.items():
            if v > 0 and self.waited[eng].get(k, 0) < v:
                self.eng[eng].wait_ge(self.semobj[k], v)
        for e in self.eng:
            if self.cnt[e] > 0 and e != eng:
                self.eng[eng].wait_ge(self.semobj[self.sem[e]], self.cnt[e])


STOP = [99]


def build_program(n_pool):
    nc = bass.Bass("TRN2", target_bir_lowering=False)

    def din(name, shape, dt=F32):
        return nc.dram_tensor(name, list(shape), dt, kind="ExternalInput").ap()

    def dout(name, shape, dt=F32):
        return nc.dram_tensor(name, list(shape), dt, kind="ExternalOutput").ap()

    xall = din("xall", [2048, D])
    xext = din("xext", [NX, D])
    gn = din("gn", [128, D]); gm = din("gm", [128, D]); gf = din("gf", [128, D])
    win_t = din("win_t", [NBLK, 128, 16, 128])
    wm_t = din("wm_t", [4, 128, 16, 256])
    wo_t = din("wo_t", [2, 128, 16, 1024])
    memx = din("memx", [256, D])
    cosT = din("cosT", [32, 2048 + NX]); sinT = din("sinT", [32, 2048 + NX])
    identd = din("identd", [128, 128]); pmd = din("pmd", [32, 32])
    aTd = din("aTd", [128, 2, 64]); wcd = din("wcd", [128, 2, 128]); wcvd = din("wcvd", [128, 4, 3])
    stcd = din("stcd", [128, 4, 2])
    trid = din("trid", [128, 128]); winmd = din("winmd", [128, 8, 5, 128])
    cmpcd = din("cmpcd", [128, 8, 3, 32]); ealld = din("ealld", [32, 16, 128])
    cwin = din("cwin", [512, 512]); cmem = din("cmem", [256, 1024])
    cache2 = din("cache2", [n_pool * 256, 512])
    ptrepd = din("ptrepd", [128, 128], I32); ptcold = din("ptcold", [128, 1], I32)
    rowAd = din("rowAd", [128, 1]); colBd = din("colBd", [128, 128]); arepd = din("arepd", [128, 512])
    ind2d = din("ind2d", [128, 2]); rseld = din("rseld", [16, 4]); maskw16d = din("maskw16d", [128, 4, 16]); maskn16d = din("maskn16d", [4, 16])
    rselTd = din("rselTd", [4, 16]); onehd = din("onehd", [16, 6, 24])

    y_own = dout("y_own", [1024, D]); y_s = dout("y_s", [4, D])
    kv_own = dout("kv_own", [1024, 1024]); win_own = dout("win_own", [512, 512])
    conv_p = dout("conv_p", [2, 512]); memkv = dout("memkv", [256, 1024])
    kv_s = dout("kv_s", [4, 1024]); win_s = dout("win_s", [512, 512]); conv_s = dout("conv_s", [2, 512])

    with ExitStack() as es0:
        S = Sched(nc, es0)
        cur = [es0]

        def sb(name, shape, dt=F32):
            return cur[0].enter_context(nc.sbuf_tensor(name, list(shape), dt))

        ps = [es0.enter_context(nc.psum_tensor("ps%d" % i, [128, 512], F32)) for i in range(8)]
        psB = [Buf(True) for _ in range(8)]

        def psb16(i):
            return ps[i][:, :].bitcast(BF16)

        csem = S.newsem("csem")

        def const(name, src, shape, dt=F32, q="sp"):
            t = sb(name, shape, dt)
            b = Buf()
            S.dma(q, S.newsem("c_" + name), t[:], src, writes=[b])
            return t, b

        def mm(out, lhsT, rhs, start, stop, R, W, skip=False):
            S.op("pe", lambda e: e.matmul(out, lhsT=lhsT, rhs=rhs, start=start, stop=stop, skip_group_check=skip),
                 reads=R, writes=W)

        def tr(out, in_, ident, R, W):
            S.op("pe", lambda e: e.transpose(out=out, in_=in_, identity=ident), reads=R, writes=W)

        def act(out, in_, func, R, W, **kw):
            S.op("act", lambda e: e.activation(out=out, in_=in_, func=func, **kw), reads=R, writes=W)

        def tt(out, in0, in1, op, R, W, eng="dve"):
            S.op(eng, lambda e: e.tensor_tensor(out=out, in0=in0, in1=in1, op=op), reads=R, writes=W)

        def tsc(out, in0, s1, s2, op0, op1, R, W, eng="dve"):
            if s2 is None:
                S.op(eng, lambda e: e.tensor_scalar(out, in0, s1, None, op0=op0), reads=R, writes=W)
            else:
                S.op(eng, lambda e: e.tensor_scalar(out, in0, s1, s2, op0=op0, op1=op1), reads=R, writes=W)

        def stt(out, in0, scalar, in1, op0, op1, R, W, eng="dve"):
            S.op(eng, lambda e: e.scalar_tensor_tensor(out=out, in0=in0, scalar=scalar, in1=in1, op0=op0, op1=op1),
                 reads=R, writes=W)

        def cp(out, in_, R, W, eng="dve"):
            S.op(eng, lambda e: e.tensor_copy(out, in_), reads=R, writes=W)

        identf, identfB = const("identf", identd, [128, 128])
        identb, identbB = const("identb", identd, [128, 128], BF16, "pool")
        mixedT = sb("hT_oth", [128, 16, 1024], BF16); mixB = Buf()
        mixs = sb("mixs", [128, 16, 4], BF16); mixsB = Buf()
        hTx = sb("hTx", [128, 16, NX], BF16); hTxB = Buf()
        st = sb("st", [128, 32, 4]); stB = [Buf() for _ in range(32)]
        arBs = sb("arBs", [128, 16, 4], BF16); arBsB = Buf()
        qsT = sb("qsT", [128, 8, 4], BF16); qsB = Buf()
        qrs = sb("qrs", [128, 8, 4], BF16); qrsB = Buf()
        kvsT = sb("kvsT", [128, 12, 4]); kvsB = Buf()
        bgTs = sb("bgTs", [8, 24]); bgTsB = Buf()
        kcTs = sb("kcTs", [128, 2, 256], BF16); vcs = sb("vcs", [128, 2, 2, 128], BF16); kcsB = Buf()
        rvr = [sb("rvr%d" % i, [128, 4]) for i in range(4)]; rvrB = [Buf() for _ in range(4)]
        n_rv = [0]
        xsem = [S.newsem("xs0"), S.newsem("xs1"), S.newsem("xs2")]
        wtsem = [S.newsem("wt%d" % i) for i in range(3)]
        stsem = [S.newsem("st%d" % i) for i in range(3)]
        msem = S.newsem("msem")

        esA = ExitStack(); esA.__enter__(); cur[0] = esA
        hT_own = sb("hT_own", [128, 16, 1024], BF16); hT_ownB = Buf()
        hT = [hT_own, mixedT]; hTB = [hT_ownB, mixB]
        QT = sb("QT", [128, 8, 1024], BF16); QTB = [Buf() for _ in range(8)]
        memKT = sb("memKT", [128, 4, 256], BF16); memKB = Buf()
        memV = sb("memV", [128, 2, 4, 130], BF16); memVB = Buf()

        esB = ExitStack(); esB.__enter__(); cur[0] = esB
        KT = sb("KT", [128, 2, 2, 2048], BF16); KTB = Buf()
        Vt = sb("Vt", [128, 16, 2, 2, 130], BF16); VtB = Buf()
        Sc = sb("Sc", [128, 8, 8, 32]); ScB = Buf()
        arB = sb("arB", [128, 8, 1024], BF16); arBB = [Buf() for _ in range(8)]
        bgT = sb("bgT", [128, 8, 24]); bgTB = Buf()
        pooledT = sb("pooledT", [128, 2, 2, 32]); pooledB = Buf()
        kcT = sb("kcT", [128, 2, 32], BF16); vcm = sb("vcm", [32, 2, 128], BF16); kcB = Buf()
        cmpc, cmpcB = const("cmpc", cmpcd, [128, 8, 3, 32])
        selTA = sb("selTA", [32, 16, 128], BF16); selTAB = [Buf() for _ in range(16)]
        ScU = [Buf() for _ in range(16)]
        S.op("dve", lambda e: e.memset(Vt[:, :, :, :, 128:130], 1.0), writes=[VtB])
        S.op("dve", lambda e: e.memset(memV[:, :, :, 128:130], 1.0), writes=[memVB])
        S.op("dve", lambda e: e.memset(mixs[:], 0.0), writes=[mixsB])

        e0 = ExitStack(); e0.__enter__(); cur[0] = e0
        gbc, gbcB = const("gbc", gn, [128, D])
        junk = sb("junk", [128, D], BF16)
        xs = [sb("xs0", [128, D]), sb("xs1", [128, D]), sb("xs2", [128, D])]; xsB = [Buf(), Buf(), Buf()]
        NXS = 3
        hb = [sb("hb0", [128, D], BF16), sb("hb1", [128, D], BF16)]; hbB = [Buf(), Buf()]

        def norm_stats(i, x_ap, xB_, npart):
            act(junk[:npart, :], x_ap, AF.Square, [xB_], [stB[i]], accum_out=st[:npart, i, 0:1])
            tsc(st[:npart, i, 1:2], st[:npart, i, 0:1], 1.0 / D, EPS, ALU.mult, ALU.add, [stB[i]], [stB[i]])
            act(st[:npart, i, 2:3], st[:npart, i, 1:2], AF.Sqrt, [stB[i]], [stB[i]])
            S.op("dve", lambda e: e.reciprocal(out=st[:npart, i, 3:4], in_=st[:npart, i, 2:3]),
                 reads=[stB[i]], writes=[stB[i]])

        def norm_dma(i, src_ap, npart):
            j = i % NXS
            S.dma("sp", xsem[j], xs[j][:npart, :], src_ap, writes=[xsB[j]])

        def norm_st(i, npart):
            j = i % NXS
            norm_stats(i, xs[j][:npart, :], xsB[j], npart)

        def norm_stt(i, npart):
            j = i % 2
            jx = i % NXS
            stt(hb[j][:npart, :], xs[jx][:npart, :], st[:npart, i, 3:4], gbc[:npart, :], ALU.mult, ALU.mult,
                [xsB[jx], stB[i], gbcB], [hbB[j]])

        def norm_tp(i, npart, dst_fn, dstB, mid=None):
            j = i % 2
            for half in range(2):
                bank = 6 + half
                for kk in range(8):
                    k = half * 8 + kk
                    tr(psb16(bank)[:, kk * 128:kk * 128 + npart], hb[j][:npart, k * 128:(k + 1) * 128],
                       identb[:npart, :npart], [hbB[j], identbB], [psB[bank]])
            if mid is not None:
                mid()
            for half in range(2):
                bank = 6 + half
                src3 = psb16(bank).rearrange("p (k t) -> p k t", k=8)[:, :, 0:npart]
                if half == 0:
                    act(dst_fn(half), src3, AF.Copy, [psB[bank]], [dstB])
                else:
                    cp(dst_fn(half), src3, [psB[bank]], [dstB])

        def norm_tile(i, src_ap, npart, dst_fn, dstB):
            norm_dma(i, src_ap, npart)
            norm_st(i, npart)
            norm_stt(i, npart)
            norm_tp(i, npart, dst_fn, dstB)

        jobs = []
        for i in range(16):
            hh, t0 = i // 8, (i % 8) * 128
            jobs.append((i, xall[i * 128:(i + 1) * 128, :], 128,
                         (lambda half, hh=hh, t0=t0: hT[hh][:, half * 8:(half + 1) * 8, t0:t0 + 128]), hTB[hh]))
        jobs.append((16, xext[:, :], NX, (lambda half: hTx[:, half * 8:(half + 1) * 8, :]), hTxB))
        for n_ in range(2):
            norm_dma(jobs[n_][0], jobs[n_][1], jobs[n_][2])
        for n_ in range(2):
            norm_st(jobs[n_][0], jobs[n_][2])
        norm_stt(jobs[0][0], jobs[0][2])
        for n_, (i, src_ap, npart, dst_fn, dstB) in enumerate(jobs):
            if n_ + 2 < len(jobs):
                norm_dma(jobs[n_ + 2][0], jobs[n_ + 2][1], jobs[n_ + 2][2])
            mid = None
            if n_ + 1 < len(jobs):
                mid = (lambda q=jobs[n_ + 1]: norm_stt(q[0], q[2]))
            norm_tp(i, npart, dst_fn, dstB, mid)
            if n_ + 2 < len(jobs):
                norm_st(jobs[n_ + 2][0], jobs[n_ + 2][2])

        S.dma("sp", csem, gbc[:], gm, writes=[gbcB])
        hmT = sb("hmT", [128, 16, 256], BF16); hmTB = Buf()
        for i in range(2):
            norm_tile(17 + i, memx[i * 128:(i + 1) * 128, :], 128,
                      lambda half, i=i: hmT[:, half * 8:(half + 1) * 8, i * 128:(i + 1) * 128], hmTB)
        wmr0_ = xs[1][:, :].bitcast(BF16).rearrange("p (k c) -> p k c", k=16); wmr = [wmr0_, wmr0_]; wmrB = [xsB[1], xsB[1]]
        mst0_ = sb("mst0", [128, 256]); mst = [mst0_, mst0_]; mst0B_ = Buf(); mstB = [mst0B_, mst0B_]
        mkb = sb("mkb", [128, 256], BF16); mkbB = Buf()
        n_m = 0
        for cg in range(4 if STOP[0] > -10 else 1):
            j = cg % 2
            S.dma("pool", wtsem[0], wmr[j], wm_t[cg], writes=[wmrB[j]])
            for mt in range(2):
                bank = mt
                for k in range(16):
                    mm(ps[bank][:, 0:256], hmT[:, k, mt * 128:(mt + 1) * 128], wmr[j][:, k, :], k == 0, k == 15,
                       [hmTB, wmrB[j]], [psB[bank]])
                jj = n_m % 2
                n_m += 1
                act(mst[jj][:], ps[bank][:, 0:256], AF.Copy, [psB[bank]], [mstB[jj]])
                if STOP[0] != -21:
                    S.dma("sp", stsem[0], memkv[mt * 128:(mt + 1) * 128, cg * 256:(cg + 1) * 256], mst[jj][:],
                          reads=[mstB[jj]])
                if STOP[0] == -31:
                    continue
                if cg < 2:
                    cp(mkb[:], ps[bank][:, 0:256], [psB[bank]], [mkbB])
                    for h2 in range(2):
                        tr(psb16(6)[:, h2 * 128:(h2 + 1) * 128], mkb[:, h2 * 128:(h2 + 1) * 128], identb[:],
                           [mkbB, identbB], [psB[6]])
                    cp(memKT[:, 2 * cg:2 * cg + 2, mt * 128:(mt + 1) * 128],
                       psb16(6)[:, 0:256].rearrange("p (h t) -> p h t", h=2), [psB[6]], [memKB])
                else:
                    cp(memV[:, mt, 2 * (cg - 2):2 * (cg - 2) + 2, 0:128],
                       ps[bank][:, 0:256].rearrange("p (h t) -> p h t", h=2), [psB[bank]], [memVB])
        S.barrier()
        e0.close()
        if STOP[0] == 0 or STOP[0] <= -10:
            S.final_wait("sp")
            esB.close(); esA.close()
            return nc

        e1 = ExitStack(); e1.__enter__(); cur[0] = e1
        pmf, pmfB = const("pmf", pmd, [32, 32])
        cosS, cosB = const("cosS", cosT, [32, 2048 + NX], BF16, "pool")
        sinS, sinB = const("sinS", sinT, [32, 2048 + NX], BF16, "pool")
        aT, aTB = const("aT", aTd, [128, 2, 64])
        wc, wcB = const("wc", wcd, [128, 2, 128])
        wcv, wcvB = const("wcv", wcvd, [128, 4, 3])
        stc, stcB = const("stc", stcd, [128, 4, 2])
        wt = [sb("wt%d" % i, [128, 16, 128], BF16) for i in range(3)]; wtB = [Buf() for _ in range(3)]
        kf = [sb("kf0", [128, 1024 + NX]), sb("kf1", [128, 1024 + NX])]; kfB = [Buf(), Buf()]
        tmpf = sb("tmpf", [128, 1024 + NX]); tmpB = Buf()
        vb16 = sb("vb16", [128, 1024], BF16); vb16B = Buf()
        ost = [sb("ost%d" % i, [128, 4, 128]) for i in range(2)]; ostB = [Buf() for _ in range(2)]
        uext = sb("uext", [128, 1024 + 2]); uextB = Buf()
        usx = sb("usx", [128, 8]); usxB = Buf()
        chf = sb("chf", [128, 1024 + NX]); chfB = Buf()
        tmp2 = chf; tmp2B = chfB
        cvo = sb("cvo", [128, 2, 4, 2]); cvoB = Buf()
        bgf = chf; bgfB = chfB
        n_ost = [0]; bankset = [0]; nkf = [0]
        e4 = sb("e4", [128, 4, 32]); e4B = Buf()
        sm = sb("sm", [128, 16]); smB = Buf()
        imp = sb("imp", [128, 3, 32]); impB = Buf()
        m8 = sb("m8", [128, 16]); m8B = Buf()

        def cmp_prepass(s, g):
            u = s * 2 + g
            pc = Sc[:, s, 4 * g:4 * g + 4, :]
            tt(e4[:], pc, cmpc[:, s, 0, :].unsqueeze(1).to_broadcast([128, 4, 32]), ALU.add, [ScB, cmpcB], [e4B])
            act(e4[:], e4[:], AF.Exp, [e4B], [e4B], scale=SCALE)
            S.op("dve", lambda e: e.tensor_reduce(out=sm[:, 0:4], in_=e4[:], axis=AX.X, op=ALU.add), reads=[e4B], writes=[smB])
            tsc(sm[:, 0:4], sm[:, 0:4], 1e-30, None, ALU.add, None, [smB], [smB])
            S.op("dve", lambda e: e.reciprocal(out=sm[:, 4:8], in_=sm[:, 0:4]), reads=[smB], writes=[smB])
            tt(pc, e4[:], sm[:, 4:8].unsqueeze(2).to_broadcast([128, 4, 32]), ALU.mult, [e4B, smB, ScB], [ScU[u]])
            S.op("dve", lambda e: e.tensor_reduce(out=imp[:, 0, :], in_=pc.rearrange("p h m -> p m h"), axis=AX.X,
                                                  op=ALU.add), reads=[ScU[u]], writes=[impB])
            tt(imp[:, 0, :], imp[:, 0, :], cmpc[:, s, 1, :], ALU.mult, [impB, cmpcB], [impB])
            tt(imp[:, 0, :], imp[:, 0, :], cmpc[:, s, 2, :], ALU.add, [impB, cmpcB], [impB])
            S.op("dve", lambda e: e.max(out=m8[:, 0:8], in_=imp[:, 0, :]), reads=[impB], writes=[m8B])
            S.op("dve", lambda e: e.match_replace(out=imp[:, 1, :], in_to_replace=m8[:, 0:8], in_values=imp[:, 0, :],
                                                  imm_value=-2.0), reads=[impB, m8B], writes=[impB])
            S.op("dve", lambda e: e.max(out=m8[:, 8:16], in_=imp[:, 1, :]), reads=[impB], writes=[m8B])
            tsc(imp[:, 2, :], imp[:, 0, :], m8[:, 15:16], None, ALU.is_ge, None, [impB, m8B], [impB])
            tr(ps[7][0:32, 0:128], imp[:, 2, :], identf[:, :], [impB, identfB], [psB[7]])
            cp(selTA[:, u, :], ps[7][0:32, 0:128], [psB[7]], [selTAB[u]])


        def out_rows(src, srcB, dst):
            S_ = n_ost[0] % 2
            n_ost[0] += 1
            ob_ = 6 + S_
            for t4 in range(4):
                tr(ps[ob_][:, t4 * 128:(t4 + 1) * 128], src[:, t4 * 128:(t4 + 1) * 128], identf[:],
                   [srcB, identfB], [psB[ob_]])
            act(ost[S_][:], ps[ob_][:, :].rearrange("p (t d) -> p t d", t=4), AF.Copy, [psB[ob_]], [ostB[S_]])
            S.dma("sp", stsem[S_], dst, ost[S_][:], reads=[ostB[S_]])

        def rope(buf, bufB, ncols, tabcol0):
            for c0 in range(0, ncols, 512):
                n = min(512, ncols - c0)
                mm(ps[6][0:32, 0:n], pmf[:, :], buf[0:32, c0:c0 + n], True, True, [bufB, pmfB], [psB[6]])
                tt(tmp2[0:32, c0:c0 + n], ps[6][0:32, 0:n], sinS[:, tabcol0 + c0:tabcol0 + c0 + n], ALU.mult,
                   [psB[6], sinB], [tmp2B])
            tt(buf[0:32, 0:ncols], buf[0:32, 0:ncols], cosS[:, tabcol0:tabcol0 + ncols], ALU.mult, [bufB, cosB], [bufB])
            tt(buf[0:32, 0:ncols], buf[0:32, 0:ncols], tmp2[0:32, 0:ncols], ALU.add, [bufB, tmp2B], [bufB])

        blocks = block_list()
        pend_epi = []

        def epilogue(kind, a0, a1, half, b0, b1, bx):
            def evac(dst, dstB, func=AF.Copy, rows=128, withx=True):
                act(dst[0:rows, 0:512], ps[b0][0:rows, :], func, [psB[b0]], [dstB])
                act(dst[0:rows, 512:1024], ps[b1][0:rows, :], func, [psB[b1]], [dstB])
                if withx:
                    act(dst[0:rows, 1024:1024 + NX], ps[bx][0:rows, 0:NX], func, [psB[bx]], [dstB])

            if kind == "kv":
                part, g = a0, a1
                kfi = nkf[0] % 2
                nkf[0] += 1
                K_, KB_ = kf[kfi], kfB[kfi]
                evac(K_, KB_, withx=(half == 0))
                if part in (2, 4):
                    if half == 0:
                        rope(K_, KB_, 1024 + NX, 0)
                    else:
                        rope(K_, KB_, 1024, 1024 + NX)
                    act(KT[:, (part - 2) // 2, g, half * 1024:(half + 1) * 1024], K_[:, 0:1024], AF.Copy, [KB_], [KTB])
                if part in (3, 5):
                    cp(vb16[:], K_[:, 0:1024], [KB_], [vb16B])
                    for t8 in range(8):
                        tr(psb16(7)[:, t8 * 128:(t8 + 1) * 128], vb16[:, t8 * 128:(t8 + 1) * 128], identb[:],
                           [vb16B, identbB], [psB[7]])
                    cp(Vt[:, half * 8:(half + 1) * 8, (part - 3) // 2, g, 0:128],
                       psb16(7).rearrange("p (t d) -> p t d", t=8), [psB[7]], [VtB])
                if part in (0, 1):
                    for c0 in (0, 512):
                        tt(tmpf[:, c0:c0 + 512].rearrange("p (n j) -> p n j", j=64),
                           K_[:, c0:c0 + 512].rearrange("p (n j) -> p n j", j=64),
                           aT[:, part, :].unsqueeze(1).to_broadcast([128, 8, 64]), ALU.mult, [KB_, aTB], [tmpB])
                    S.op("dve", lambda e: e.tensor_reduce(
                        out=pooledT[:, part, g, half * 16:(half + 1) * 16],
                        in_=tmpf[:, 0:1024].rearrange("p (n j) -> p n j", j=64), axis=AX.X, op=ALU.add),
                         reads=[tmpB], writes=[pooledB])
                if half == 0:
                    cp(kvsT[:, part * 2 + g, :], K_[:, 1026:1030], [KB_], [kvsB])
                    cidx = part * 2 + g
                    if part < 4:
                        for c0 in (0, 512):
                            out_rows(K_[:, c0:c0 + 512], KB_,
                                     kv_own[c0:c0 + 512, cidx * 128:(cidx + 1) * 128].rearrange("(t p) d -> p t d", p=128))
                    else:
                        out_rows(K_[:, 512:1024], KB_,
                                 win_own[:, (cidx - 8) * 128:(cidx - 7) * 128].rearrange("(t p) d -> p t d", p=128))
            elif kind == "ch":
                evac(chf, chfB)
            elif kind == "cc":
                cb_ = a0
                for c0, bank in ((0, b0), (512, b1)):
                    tt(uext[:, 2 + c0:2 + c0 + 512], ps[bank][:, :], chf[:, c0:c0 + 512], ALU.mult,
                       [psB[bank], chfB], [uextB])
                tt(usx[:, 0:NX], ps[bx][:, 0:NX], chf[:, 1024:1024 + NX], ALU.mult, [psB[bx], chfB], [usxB])
                cp(uext[:, 0:2], usx[:, 0:2], [usxB], [uextB])
                cp(usx[:, 0:2], stc[:, cb_, :], [stcB, uextB], [usxB])
                cp(cvo[:, 0, cb_, :], uext[:, 1024:1026], [uextB], [cvoB])
                cp(cvo[:, 1, cb_, :], usx[:, 4:6], [usxB], [cvoB])
                for (u_, uB_, n_, o_) in ((uext, uextB, 1024, 0), (usx, usxB, 4, 1024)):
                    tsc(tmpf[:, o_:o_ + n_], u_[:, 0:n_], wcv[:, cb_, 0:1], None, ALU.mult, None, [uB_, wcvB], [tmpB])
                    for jj in (1, 2):
                        stt(tmpf[:, o_:o_ + n_], u_[:, jj:jj + n_], wcv[:, cb_, jj:jj + 1], tmpf[:, o_:o_ + n_],
                            ALU.mult, ALU.add, [uB_, wcvB, tmpB], [tmpB])
            elif kind == "cb":
                for c0, bank in ((0, b0), (512, b1)):
                    tt(tmpf[:, c0:c0 + 512], tmpf[:, c0:c0 + 512], ps[bank][:, :], ALU.mult, [tmpB, psB[bank]], [tmpB])
                tt(tmpf[:, 1024:1028], tmpf[:, 1024:1028], ps[bx][:, 2:6], ALU.mult, [tmpB, psB[bx]], [tmpB])
            elif kind == "cg":
                cb_ = a0
                evac(chf, chfB, func=AF.Silu)
                tt(mixedT[:, cb_, :], tmpf[:, 0:1024], chf[:, 0:1024], ALU.mult, [tmpB, chfB], [mixB])
                tt(mixs[:, cb_, :], tmpf[:, 1024:1028], chf[:, 1026:1030], ALU.mult, [tmpB, chfB], [mixsB])
            elif kind == "q":
                h = a0
                g = h // 4
                kfi = nkf[0] % 2
                nkf[0] += 1
                K_, KB_ = kf[kfi], kfB[kfi]
                evac(K_, KB_)
                act(QT[:, h, :], K_[:, 0:1024], AF.Copy, [KB_], [QTB[h]])
                act(qsT[:, h, :], K_[:, 1026:1030], AF.Copy, [KB_], [qsB])
                for s in range(8):
                    mm(ps[7][:, s * 32:(s + 1) * 32], QT[:, h, s * 128:(s + 1) * 128], kcT[:, g, :], True, True,
                       [QTB[h], kcB], [psB[7]])
                cp(Sc[:, :, h, :], ps[7][:, 0:256].rearrange("p (s m) -> p s m", s=8), [psB[7]], [ScB])
                rope(K_, KB_, 1024 + NX, 0)
                act(QT[0:32, h, :], K_[0:32, 0:1024], AF.Copy, [KB_], [QTB[h]])
                act(qrs[:, h, :], K_[:, 1026:1030], AF.Copy, [KB_], [qrsB])
            elif kind == "ng":
                h = a0
                act(arB[:, h, 0:512], ps[b0][:, :], AF.Silu, [psB[b0]], [arBB[h]])
                act(arB[:, h, 512:1024], ps[b1][:, :], AF.Silu, [psB[b1]], [arBB[h]])
                act(arBs[:, h, :], ps[bx][:, 2:6], AF.Silu, [psB[bx]], [arBsB])
                cmp_prepass(h, 0)
                cmp_prepass(h, 1)
            elif kind == "bg":
                evac(bgf, bgfB, func=AF.Sigmoid, rows=24)
                for s in range(8):
                    tr(ps[7][:, s * 24:(s + 1) * 24], bgf[0:24, s * 128:(s + 1) * 128], identf[0:24, 0:24],
                       [bgfB, identfB], [psB[7]])
                cp(bgT[:, :, :], ps[7][:, 0:192].rearrange("p (s c) -> p s c", s=8), [psB[7]], [bgTB])
                tr(ps[7][0:4, 256:280], bgf[0:24, 1026:1030], identf[0:24, 0:24], [bgfB, identfB], [psB[7]])
                cp(bgTs[0:4, :], ps[7][0:4, 256:280], [psB[7]], [bgTsB])

            if kind == "kv" and a0 == 1 and a1 == 1 and half == 1:
                for g in range(2):
                    mm(ps[6][:, 0:32], wc[:, 0, :], pooledT[:, 0, g, :], True, True, [wcB, pooledB], [psB[6]])
                    cp(kcT[:, g, :], ps[6][:, 0:32], [psB[6]], [kcB])
                    mm(ps[6][0:32, 0:128], pooledT[:, 1, g, :], wc[:, 1, :], True, True, [wcB, pooledB], [psB[6]])
                    cp(vcm[:, g, :], ps[6][0:32, 0:128], [psB[6]], [kcB])


        for bi, (kind, a0, a1, col) in enumerate(blocks):
            if kind in ("mq", "mg"):
                continue
            ws = bi % 3
            S.dma("pool", wtsem[ws], wt[ws][:], win_t[bi], writes=[wtB[ws]])
            passes = [0, 1] if kind == "kv" else [0]
            for half in passes:
                bs = bankset[0]
                bankset[0] = 3 - bs
                b0, b1, bx = bs, bs + 1, bs + 2
                for k in range(16):
                    for gi, bank in ((0, b0), (1, b1)):
                        mm(ps[bank][:, :], wt[ws][:, k, :], hT[half][:, k, gi * 512:(gi + 1) * 512], k == 0, k == 15,
                           [wtB[ws], hTB[half]], [psB[bank]])
                    if half == 0:
                        mm(ps[bx][:, 0:NX], wt[ws][:, k, :], hTx[:, k, :], k == 0, k == 15, [wtB[ws], hTxB], [psB[bx]])

                if pend_epi:
                    epilogue(*pend_epi.pop(0))
                pend_epi.append((kind, a0, a1, half, b0, b1, bx))
        while pend_epi:
            epilogue(*pend_epi.pop(0))
        with nc.allow_non_contiguous_dma(reason="tiny transposed conv-state store"):
            for cb_ in range(4):
                S.dma("sp", msem, conv_p[:, cb_ * 128:(cb_ + 1) * 128].rearrange("j p -> p j"), cvo[:, 0, cb_, :], reads=[cvoB])
                S.dma("sp", msem, conv_s[:, cb_ * 128:(cb_ + 1) * 128].rearrange("j p -> p j"), cvo[:, 1, cb_, :], reads=[cvoB])
        for grp in range(3):
            for b4 in range(4):
                tr(ps[7][0:4, b4 * 128:(b4 + 1) * 128], kvsT[:, grp * 4 + b4, :], identf[:, :], [kvsB, identfB], [psB[7]])
            cp(tmpf[0:4, 0:512], ps[7][0:4, :], [psB[7]], [tmpB])
            dsts = kv_s[:, grp * 512:(grp + 1) * 512] if grp < 2 else win_s[508:512, :]
            S.dma("sp", msem, dsts, tmpf[0:4, 0:512], reads=[tmpB])
        S.dma("sp", msem, win_s[0:508, :], cwin[4:512, :])
        S.barrier()
        e1.close()
        if STOP[0] == 1:
            S.final_wait("sp")
            esB.close(); esA.close()
            return nc

        e2 = ExitStack(); e2.__enter__(); cur[0] = e2
        tri, triB = const("tri", trid, [128, 128], BF16, "pool")
        winm, winmB = const("winm", winmd, [128, 8, 5, 128], BF16, "pool")
        eall, eallB = const("eall", ealld, [32, 16, 128], BF16, "pool")
        PT = [sb("PT0", [128, 512], BF16), sb("PT1", [128, 512], BF16)]; PTB = [Buf(), Buf()]
        n_att = [0]
        wc2, wc2B = const("wc2", wcd, [128, 2, 128])
        arep, arepB = const("arep", arepd, [128, 512])
        ind2, ind2B = const("ind2", ind2d, [128, 2], BF16, "pool")
        ptrep, ptrepB = const("ptrep", ptrepd, [128, 128], I32)
        rowA, rowAB = const("rowA", rowAd, [128, 1])
        idxfA = sb("idxfA", [128, 128]); idxfAB = Buf()
        idxA = sb("idxA", [128, 128], I32); idxAB = Buf()
        cp(idxfA[:], ptrep[:], [ptrepB], [idxfAB])
        tsc(idxfA[:], idxfA[:], 256.0, rowA[:, 0:1], ALU.mult, ALU.add, [idxfAB, rowAB], [idxfAB])
        cp(idxA[:], idxfA[:], [idxfAB], [idxAB])
        NR = 4
        ct = [sb("ct%d" % i, [128, 512]) for i in range(NR)]; ctB = [Buf() for _ in range(NR)]
        ctsem = [S.newsem("ct%d" % i) for i in range(NR)]
        prod = [sb("prod%d" % i, [128, 512], BF16) for i in range(2)]; prodB = [Buf(), Buf()]
        pl = sb("pl", [128, 4, 128, 2]); plB = Buf()
        p1 = dict(g=0, m=0, p=0)

        def pass1_step():
            if p1["g"] < 128 and p1["g"] - p1["p"] < NR:
                s_ = p1["g"]
                S.dma("pool", ctsem[s_ % NR], ct[s_ % NR][:], cache2, reads=[idxAB], writes=[ctB[s_ % NR]],
                      indirect=idxA[:, s_:s_ + 1])
                p1["g"] += 1
            if p1["m"] < p1["g"] - 1 or (p1["g"] == 128 and p1["m"] < 128):
                s_ = p1["m"]
                tt(prod[s_ % 2][:], ct[s_ % NR][:], arep[:], ALU.mult, [ctB[s_ % NR], arepB], [prodB[s_ % 2]])
                p1["m"] += 1
            if p1["p"] < p1["m"] - 1 or (p1["m"] == 128 and p1["p"] < 128):
                s_ = p1["p"]
                for pg in range(4):
                    c0 = ((s_ % 64) * 4 + pg) * 2
                    mm(ps[7][:, c0:c0 + 2], prod[s_ % 2][:, pg * 128:(pg + 1) * 128], ind2[:, :], True, True,
                       [prodB[s_ % 2], ind2B], [psB[7]])
                if s_ % 64 == 63:
                    s0 = s_ - 63
                    cp(pl[:, :, s0:s0 + 64, :], ps[7][:, :].rearrange("p (s g c) -> p g s c", s=64, g=4), [psB[7]], [plB])
                p1["p"] += 1


        pend = []
        p1_on = [False]
        sbase = [4]

        def att_B(nh, nq, i, t, ob, first, last):
            nk = t["nk"]
            for h in range(nh):
                bank = ob[h // 2]
                off = t.get("ooff", (h % 2) * 130)
                mm(ps[bank][0:nq, off:off + 129], PT[i][0:nk, h * nq:(h + 1) * nq], t["va"],
                   first and h % 2 == 0, last, [PTB[i]] + t["vR"], [psB[bank]], skip=True)

        after = []

        def att_flush():
            while pend:
                att_B(*pend.pop(0))
            while after:
                after.pop(0)()

        def att_tile(nh, nq, qrhs, qR, t, ob, first, last):
            if callable(t):
                t = t()
            W = nh * nq
            i = n_att[0] % 2
            n_att[0] += 1
            nk = t["nk"]
            m = None
            if t.get("mask") is not None:
                m = t["mask"]() if callable(t["mask"]) else t["mask"]
            sb_ = sbase[0] + i
            so = ps[sb_][0:nk, 0:W]
            if len(qrhs.shape) == 3:
                so = so.rearrange("p (h q) -> p h q", h=nh)
            mm(so, t["kt"], qrhs, True, True, t["kR"] + qR, [psB[sb_]])
            att_flush()
            act(PT[i][0:nk, 0:W], ps[sb_][0:nk, 0:W], AF.Exp, [psB[sb_]], [PTB[i]], scale=SCALE)
            if m is not None:
                m_ap, mR = m
                p3 = PT[i][0:nk, 0:W].rearrange("p (h q) -> p h q", h=nh)
                tt(p3, p3, m_ap.unsqueeze(1).to_broadcast([nk, nh, nq]), ALU.mult, [PTB[i]] + mR, [PTB[i]])
            pend.append((nh, nq, i, t, ob, first, last))
            if p1_on[0]:
                pass1_step()

        def attend(nh, nq, qrhs, qR, tiles, ob, flush=True):
            nt = len(tiles)
            for ti, t in enumerate(tiles):
                att_tile(nh, nq, qrhs, qR, t, ob, ti == 0, ti == nt - 1)
            if flush:
                att_flush()

        pT = sb("pT", [32, 4, 128], BF16); pTB = Buf()
        onsa = [sb("onsa0", [128, 4, 128]), sb("onsa1", [128, 4, 128])]; onsaB = [[Buf() for _ in range(4)] for _ in range(2)]
        onb = sb("onb", [128, 4, 128], BF16); onbB = Buf()
        mk = [sb("mk0", [128, 128], BF16), sb("mk1", [128, 128], BF16)]; mkB = [Buf(), Buf()]
        rv = sb("rv", [128, 8]); rvB = Buf()
        n_mk = [0]
        mkps = [Buf(), Buf()]


        def fin_branch(ob, on, onBh, gate2_fn, nq=128):
            gR = [bgTsB] + ([bgTB] if nq == 128 else [])
            rs_ = []
            for hp in range(2):
                ri = n_rv[0] % 4
                n_rv[0] += 1
                rs_.append((rvr[ri], rvrB[ri], ps[ob[hp]][0:nq, 0:260].rearrange("p (h c) -> p h c", h=2)))
            for hp, (R_, RB_, o3) in enumerate(rs_):
                S.op("dve", lambda e: e.reciprocal(out=R_[0:nq, 0:2], in_=o3[:, :, 128]), reads=[psB[ob[hp]]], writes=[RB_])
            for hp, (R_, RB_, o3) in enumerate(rs_):
                tt(R_[0:nq, 2:4], R_[0:nq, 0:2], gate2_fn(hp), ALU.mult, [RB_] + gR, [RB_])
            for hp, (R_, RB_, o3) in enumerate(rs_):
                for hh in range(2):
                    h = 2 * hp + hh
                    stt(on[0:nq, h, :], o3[:, hh, 0:128], R_[0:nq, 2 + hh:3 + hh], on[0:nq, h, :],
                        ALU.mult, ALU.add, [psB[ob[hp]], RB_, onBh[h]], [onBh[h]])

        p1_on[0] = True
        for s in range(8):
            for g in range(2):
                oi = (s * 2 + g) % 2
                on, onB_ = onsa[oi], onsaB[oi]
                bg3 = bgT[:, s, :].rearrange("p (h b) -> p h b", b=3)
                u = s * 2 + g
                pc = Sc[:, s, 4 * g:4 * g + 4, :]
                pcB = ScU[u]
                for h in range(4):
                    tr(ps[6][0:32, h * 128:(h + 1) * 128], pc[:, h, :], identf[:, :], [pcB, identfB], [psB[6]])
                cp(pT[:], ps[6][0:32, :].rearrange("p (h q) -> p h q", h=4), [psB[6]], [pTB])
                for h in range(4):
                    mm(ps[6][:, h * 128:(h + 1) * 128], pT[:, h, :], vcm[:, g, :], True, True, [pTB, kcB], [psB[6]])
                tt(on[:], ps[6][:, :].rearrange("p (h d) -> p h d", h=4),
                   bg3[:, 4 * g:4 * g + 4, 0].unsqueeze(2).to_broadcast([128, 4, 128]), ALU.mult, [psB[6], bgTB], onB_)
                qrhs = QT[:, 4 * g:4 * g + 4, s * 128:(s + 1) * 128]
                qR = [QTB[4 * g + h] for h in range(4)]
                def selmask(j):
                    def f():
                        mi = n_mk[0] % 2
                        n_mk[0] += 1
                        reg = ps[6][:, 0:128]
                        mm(reg, eall[:, j, :], selTA[:, u, :], True, True, [eallB, selTAB[u]], [psB[6]])
                        if j == s:
                            tt(mk[mi][:], reg, tri[:], ALU.mult, [psB[6], triB], [mkB[mi]])
                        else:
                            act(mk[mi][:], reg, AF.Copy, [psB[6]], [mkB[mi]])
                        return mk[mi][:], [mkB[mi]]
                    return f
                tiles = [dict(kt=KT[:, 0, g, j * 128:(j + 1) * 128], kR=[KTB], va=Vt[:, j, 0, g, 0:129], vR=[VtB],
                              nk=128, mask=selmask(j)) for j in list(range(8, 16)) + list(range(0, s + 1))]
                attend(4, 128, qrhs, qR, tiles, [0, 1], flush=False)
                after.append(lambda on=on, onB_=onB_, bg3=bg3, g=g: fin_branch([0, 1], on, onB_, lambda hp: bg3[:, 4 * g + 2 * hp:4 * g + 2 * hp + 2, 1]))
                tiles = []
                for t5, j in enumerate(win_tiles(s)):
                    interior = (j < 8) and (s - 3 <= j <= s - 1)
                    tiles.append(dict(kt=KT[:, 1, g, j * 128:(j + 1) * 128], kR=[KTB], va=Vt[:, j, 1, g, 0:129], vR=[VtB],
                                      nk=128, mask=None if interior else (winm[:, s, t5, :], [winmB])))
                attend(4, 128, qrhs, qR, tiles, [2, 3], flush=False)

                def fin_unit(on=on, onB_=onB_, bg3=bg3, g=g, s=s):
                    fin_branch([2, 3], on, onB_, lambda hp: bg3[:, 4 * g + 2 * hp:4 * g + 2 * hp + 2, 2])
                    act(onb[:], on[:], AF.Copy, onB_, [onbB])
                    for h in range(4):
                        tr(psb16(6)[:, h * 128:(h + 1) * 128], onb[:, h, :], identb[:], [onbB, identbB], [psB[6]])
                    tt(mixedT[:, 4 + 4 * g:8 + 4 * g, s * 128:(s + 1) * 128],
                       psb16(6)[:, 0:512].rearrange("p (h q) -> p h q", h=4),
                       arB[:, 4 * g:4 * g + 4, s * 128:(s + 1) * 128], ALU.mult,
                       [psB[6]] + [arBB[4 * g + h] for h in range(4)], [mixB])
                after.append(fin_unit)
        att_flush()
        p1_on[0] = False
        while p1["p"] < 128:
            pass1_step()
        plf = pl[:].rearrange("p g s c -> p g (s c)")
        for g in range(2):
            mm(ps[6][:, 0:256], wc2[:, 0, :], plf[:, g, :], True, True, [wc2B, plB], [psB[6]])
            cp(kcTs[:, g, :], ps[6][:, 0:256], [psB[6]], [kcsB])
            for nt_ in range(2):
                mm(ps[7][:, 0:128], plf[:, 2 + g, nt_ * 128:(nt_ + 1) * 128], wc2[:, 1, :], True, True, [wc2B, plB], [psB[7]])
                cp(vcs[:, nt_, g, :], ps[7][:, 0:128], [psB[7]], [kcsB])
        S.barrier()
        e2.close()
        esB.close()
        if STOP[0] == 2:
            S.final_wait("sp")
            esA.close()
            return nc

        e3 = ExitStack(); e3.__enter__(); cur[0] = e3
        wt = [sb("wt3_%d" % i, [128, 16, 128], BF16) for i in range(2)]; wtB = [Buf() for _ in range(2)]
        PT = [sb("PT0b", [128, 512], BF16), sb("PT1b", [128, 512], BF16)]; PTB = [Buf(), Buf()]
        om = sb("om", [128, 4, 128], BF16); omB = Buf()
        arM = sb("arM", [128, 4, 1024], BF16); arMB = [Buf() for _ in range(4)]
        rv = sb("rvb", [128, 8]); rvB = Buf()
        cmb = sb("cmb", [128, 2, 1024], BF16); cmbB = Buf()
        memKTs = sb("memKTs", [128, 4, 256], BF16); memKsB = Buf()
        memVs = sb("memVs", [128, 2, 4, 130], BF16); memVsB = Buf()
        oms = sb("oms", [4, 4, 128], BF16); omsB = Buf()
        S.dma("pool", S.newsem("c_cmb"), cmb[:], cmem.rearrange("(t p) c -> p t c", p=128), writes=[cmbB])
        S.op("dve", lambda e: e.memset(memVs[:, :, :, 128:130], 1.0), writes=[memVsB])
        bankset = [0]
        for bi, (kind, a0, a1, col) in enumerate(blocks):
            if kind not in ("mq", "mg"):
                continue
            ws = bi % 2
            S.dma("pool", wtsem[ws], wt[ws][:], win_t[bi], writes=[wtB[ws]])
            bs = bankset[0]
            bankset[0] = 3 - bs
            b0, b1, bx = bs, bs + 1, bs + 2
            for k in range(16):
                for gi, bank in ((0, b0), (1, b1)):
                    mm(ps[bank][:, :], wt[ws][:, k, :], hT_own[:, k, gi * 512:(gi + 1) * 512], k == 0, k == 15,
                       [wtB[ws], hT_ownB], [psB[bank]])
                mm(ps[bx][:, 0:NX], wt[ws][:, k, :], hTx[:, k, :], k == 0, k == 15, [wtB[ws], hTxB], [psB[bx]])
            fn_ = AF.Copy if kind == "mq" else AF.Silu
            dst, dB_ = (QT, QTB[a0]) if kind == "mq" else (arM, arMB[a0])
            act(dst[:, a0, 0:512], ps[b0][:, :], fn_, [psB[b0]], [dB_])
            act(dst[:, a0, 512:1024], ps[b1][:, :], fn_, [psB[b1]], [dB_])
            act(arBs[:, a0 + (8 if kind == "mq" else 12), :], ps[bx][:, 2:6], fn_, [psB[bx]], [arBsB])
        for qg in range(2):
            for hm in range(4):
                tiles = [dict(kt=memKT[:, hm, mt * 128:(mt + 1) * 128], kR=[memKB], va=memV[:, mt, hm, 0:129], vR=[memVB],
                              nk=128, mask=None) for mt in range(2)]
                attend(4, 128, QT[:, hm, qg * 512:(qg + 1) * 512], [QTB[hm]], tiles, [0, 1])
                for hp in range(2):
                    o3 = ps[hp][:, 0:260].rearrange("p (h c) -> p h c", h=2)
                    S.op("dve", lambda e: e.reciprocal(out=rv[:, 0:2], in_=o3[:, :, 128]), reads=[psB[hp]], writes=[rvB])
                    for hh in range(2):
                        tsc(om[:, 2 * hp + hh, :], o3[:, hh, 0:128], rv[:, hh:hh + 1], None, ALU.mult, None,
                            [psB[hp], rvB], [omB])
                for qt in range(4):
                    tr(psb16(6)[:, qt * 128:(qt + 1) * 128], om[:, qt, :], identb[:], [omB, identbB], [psB[6]])
                tt(mixedT[:, 12 + hm, qg * 512:(qg + 1) * 512], psb16(6)[:, 0:512], arM[:, hm, qg * 512:(qg + 1) * 512],
                   ALU.mult, [psB[6], arMB[hm]], [mixB])
        for mt in range(2):
            for hm in range(4):
                tr(psb16(6)[:, hm * 128:(hm + 1) * 128], cmb[:, mt, hm * 128:(hm + 1) * 128], identb[:], [cmbB, identbB], [psB[6]])
            cp(memKTs[:, :, mt * 128:(mt + 1) * 128], psb16(6)[:, 0:512].rearrange("p (h t) -> p h t", h=4), [psB[6]], [memKsB])
            cp(memVs[:, mt, :, 0:128], cmb[:, mt, 512:1024].rearrange("p (h d) -> p h d", h=4), [cmbB], [memVsB])
        for hm in range(4):
            tiles = [dict(kt=memKTs[:, hm, mt * 128:(mt + 1) * 128], kR=[memKsB], va=memVs[:, mt, hm, 0:129], vR=[memVsB],
                          nk=128, mask=None) for mt in range(2)]
            attend(1, 4, arBs[:, 8 + hm, :], [arBsB], tiles, [0])
            S.op("dve", lambda e: e.reciprocal(out=rv[0:4, 0:1], in_=ps[0][0:4, 128:129]), reads=[psB[0]], writes=[rvB])
            tsc(oms[0:4, hm, :], ps[0][0:4, 0:128], rv[0:4, 0:1], None, ALU.mult, None, [psB[0], rvB], [omsB])
        for hm in range(4):
            tr(psb16(6)[:, hm * 4:(hm + 1) * 4], oms[0:4, hm, :], identb[0:4, 0:4], [omsB, identbB], [psB[6]])
        tt(mixs[:, 12:16, :], psb16(6)[:, 0:16].rearrange("p (h t) -> p h t", h=4), arBs[:, 12:16, :], ALU.mult,
           [psB[6], arBsB], [mixsB])
        S.barrier()
        e3.close()
        esA.close()
        if STOP[0] == 3:
            S.final_wait("sp")
            return nc


        e5 = ExitStack(); e5.__enter__(); cur[0] = e5
        PT = [sb("PT0s", [128, 512], BF16), sb("PT1s", [128, 512], BF16)]; PTB = [Buf(), Buf()]
        rv = sb("rvs", [128, 8]); rvB = Buf()
        rsel, rselB = const("rsel", rseld, [16, 4])
        ptcol, ptcolB = const("ptcol", ptcold, [128, 1], I32)
        colB, colBB = const("colB", colBd, [128, 128])
        idxf = sb("idxf", [128, 128]); idxfB = Buf()
        idxB = sb("idxB", [128, 128], I32); idxBB = Buf()
        pg1 = sb("pg1", [128, 2]); pg1B = Buf()
        cp(pg1[:, 0:1], ptcol[:], [ptcolB], [pg1B])
        tsc(pg1[:, 1:2], pg1[:, 0:1], 256.0, None, ALU.mult, None, [pg1B], [pg1B])
        tsc(idxf[:], colB[:], pg1[:, 1:2], None, ALU.add, None, [idxfB, colBB, pg1B], [idxfB])
        cp(idxB[:], idxf[:], [idxfB], [idxBB])

        rselT, rselTB = const("rselT", rselTd, [4, 16])
        oneh, onehB = const("oneh", onehd, [16, 6, 24])
        maskw16, maskw16B = const("maskw16", maskw16d, [128, 4, 16], BF16, "pool")
        maskn16, maskn16B = const("maskn16", maskn16d, [4, 16], BF16, "pool")
        G16 = sb("G16", [16, 8]); G16B = Buf()
        tG = sb("tG", [16, 6, 24]); tGB = Buf()
        mm(ps[6][0:16, 0:24], rselT[:, :], bgTs[0:4, :], True, True, [rselTB, bgTsB], [psB[6]])
        tt(tG[:], oneh[:], ps[6][0:16, 0:24].unsqueeze(1).to_broadcast([16, 6, 24]), ALU.mult, [onehB, psB[6]], [tGB])
        S.op("dve", lambda e: e.tensor_reduce(out=G16[:, 0:6], in_=tG[:], axis=AX.X, op=ALU.add), reads=[tGB], writes=[G16B])

        e16 = sb("e16", [16, 256]); e16B = Buf()
        sms = sb("sms", [16, 4]); smsB = Buf()
        pTs = sb("pTs", [128, 2, 16], BF16); pTsB = Buf()
        impS = sb("impS", [4, 3, 256]); impSB = Buf()
        m8s = sb("m8s", [4, 16]); m8sB = Buf()
        on16 = [sb("on16_0", [16, 128]), sb("on16_1", [16, 128])]; on16B = [Buf(), Buf()]
        maskB2 = sb("maskB2", [128, 2, 2, 4], BF16); maskB2B = Buf()
        maskB16 = sb("maskB16", [128, 2, 2, 16], BF16); maskB16B = Buf()
        e16b = sb("e16b", [16, 256], BF16); e16bB = Buf()
        qs16 = [qsT[:, 4 * g:4 * g + 4, :].rearrange("p h t -> p (h t)") for g in range(2)]
        qr16 = [qrs[:, 4 * g:4 * g + 4, :].rearrange("p h t -> p (h t)") for g in range(2)]
        for g in range(2):
            mm(ps[6][0:16, 0:256], qs16[g], kcTs[:, g, :], True, True, [qsB, kcsB], [psB[6]])
            act(e16[:], ps[6][0:16, 0:256], AF.Exp, [psB[6]], [e16B], scale=SCALE)
            S.op("dve", lambda e: e.tensor_reduce(out=sms[:, 0:1], in_=e16[:], axis=AX.X, op=ALU.add), reads=[e16B], writes=[smsB])
            S.op("dve", lambda e: e.reciprocal(out=sms[:, 1:2], in_=sms[:, 0:1]), reads=[smsB], writes=[smsB])
            tsc(e16[:], e16[:], sms[:, 1:2], None, ALU.mult, None, [e16B, smsB], [e16B])
            cp(e16b[:], e16[:], [e16B], [e16bB])
            for nt_ in range(2):
                tr(psb16(7)[:, nt_ * 16:(nt_ + 1) * 16], e16b[:, nt_ * 128:(nt_ + 1) * 128], identb[0:16, 0:16],
                   [e16bB, identbB], [psB[7]])
            cp(pTs[:], psb16(7)[:, 0:32].rearrange("p (n q) -> p n q", n=2), [psB[7]], [pTsB])
            for nt_ in range(2):
                mm(ps[7][0:16, 0:128], pTs[:, nt_, :], vcs[:, nt_, g, :], nt_ == 0, nt_ == 1, [pTsB, kcsB], [psB[7]])
            tsc(on16[g][:], ps[7][0:16, 0:128], G16[:, 3 * g:3 * g + 1], None, ALU.mult, None, [psB[7], G16B], [on16B[g]])
            mm(ps[6][0:4, 256:512], rsel[:, :], e16[:], True, True, [rselB, e16B], [psB[6]])
            cp(impS[:, 0, :], ps[6][0:4, 256:512], [psB[6]], [impSB])
            S.op("dve", lambda e: e.memset(impS[:, 0, 0:1], -1.0), reads=[impSB], writes=[impSB])
            S.op("dve", lambda e: e.memset(impS[:, 0, 255:256], -1.0), reads=[impSB], writes=[impSB])
            S.op("dve", lambda e: e.max(out=m8s[:, 0:8], in_=impS[:, 0, :]), reads=[impSB], writes=[m8sB])
            S.op("dve", lambda e: e.match_replace(out=impS[:, 1, :], in_to_replace=m8s[:, 0:8], in_values=impS[:, 0, :],
                                                  imm_value=-2.0), reads=[impSB, m8sB], writes=[impSB])
            S.op("dve", lambda e: e.max(out=m8s[:, 8:16], in_=impS[:, 1, :]), reads=[impSB], writes=[m8sB])
            tsc(impS[:, 2, :], impS[:, 0, :], m8s[:, 12:13], None, ALU.is_ge, None, [impSB, m8sB], [impSB])
            S.op("dve", lambda e: e.memset(impS[:, 2, 0:1], 1.0), reads=[impSB], writes=[impSB])
            S.op("dve", lambda e: e.memset(impS[:, 2, 255:256], 1.0), reads=[impSB], writes=[impSB])
            sel3 = impS[:, 2, :].rearrange("p (s c) -> p s c", c=2)
            for c in range(2):
                tr(ps[6][:, c * 4:(c + 1) * 4], sel3[:, :, c], identf[0:4, 0:4], [impSB, identfB], [psB[6]])
            cp(maskB2[:, :, g, :], ps[6][:, 0:8].rearrange("p (c t) -> p c t", c=2), [psB[6]], [maskB2B])
            cp(maskB16[:, :, g, :].rearrange("p c (h t) -> p c h t", h=4),
               maskB2[:, :, g, :].unsqueeze(2).to_broadcast([128, 2, 4, 4]), [maskB2B], [maskB16B])

        knew = sb("knew", [128, 2, 2, 4], BF16); knewB = Buf()
        vnew = sb("vnew", [4, 2, 2, 130], BF16); vnewB = Buf()
        S.op("dve", lambda e: e.memset(vnew[:, :, :, 128:130], 1.0), writes=[vnewB])
        for br in range(2):
            for g in range(2):
                cp(knew[:, br, g, :], kvsT[:, 4 + 4 * br + g, :], [kvsB], [knewB])
                tr(ps[7][0:4, 0:128], kvsT[:, 6 + 4 * br + g, :], identf[:, :], [kvsB, identfB], [psB[7]])
                cp(vnew[:, br, g, 0:128], ps[7][0:4, 0:128], [psB[7]], [vnewB])

        def fin16(bank, g, br, off=0):
            ri = n_rv[0] % 4
            n_rv[0] += 1
            R_, RB_ = rvr[ri], rvrB[ri]
            S.op("dve", lambda e: e.reciprocal(out=R_[0:16, 0:1], in_=ps[bank][0:16, off + 128:off + 129]),
                 reads=[psB[bank]], writes=[RB_])
            tt(R_[0:16, 1:2], R_[0:16, 0:1], G16[:, 3 * g + br:3 * g + br + 1], ALU.mult, [RB_, G16B], [RB_])
            stt(on16[g][:], ps[bank][0:16, off:off + 128], R_[0:16, 1:2], on16[g][:], ALU.mult, ALU.add,
                [psB[bank], RB_, on16B[g]], [on16B[g]])

        NSL = 8
        stl = [sb("stl%d" % i, [128, 512], BF16) for i in range(NSL)]; stlB = [Buf() for _ in range(NSL)]
        stsem2 = [S.newsem("stl%d" % i) for i in range(NSL)]
        kts = [sb("kts%d" % i, [128, 512], BF16) for i in range(2)]; ktsB = [Buf(), Buf()]
        vts = [sb("vts%d" % i, [128, 4, 2, 130], BF16) for i in range(2)]; vtsB = [Buf(), Buf()]
        for i in range(2):
            S.op("dve", lambda e: e.memset(vts[i][:, :, :, 128:130], 1.0), writes=[vtsB[i]])
        obank = [0, 2]
        pend2 = []

        def stepB(i, vi, g, r4):
            for rr in range(4):
                mm(ps[4][0:16, g * 130:g * 130 + 129], PT[i][:, rr * 16:(rr + 1) * 16], vts[vi][:, rr, g, 0:129],
                   r4 == 0 and rr == 0 and g == 0, False, [PTB[i], vtsB[vi]], [psB[4]], skip=True)

        def pass2_issue(r4):
            for rr in range(4):
                r = r4 * 4 + rr
                sl = r % NSL
                S.dma("pool", stsem2[sl], stl[sl][:], cache2, reads=[idxBB], writes=[stlB[sl]], indirect=idxB[:, r:r + 1])

        def pass2_step(r4):
            vi = r4 % 2
            for rr in range(4):
                sl = (r4 * 4 + rr) % NSL
                act(vts[vi][:, rr, :, 0:128], stl[sl][:, 256:512].rearrange("p (g d) -> p g d", g=2), AF.Copy,
                    [stlB[sl]], [vtsB[vi]])
            for g in range(2):
                ki = (r4 * 2 + g) % 2
                i = n_att[0] % 2
                n_att[0] += 1
                for rr in range(4):
                    sl = (r4 * 4 + rr) % NSL
                    tr(psb16(6 + ki)[:, rr * 128:(rr + 1) * 128], stl[sl][:, g * 128:(g + 1) * 128], identb[:],
                       [stlB[sl], identbB], [psB[6 + ki]])
                cp(kts[ki][:], psb16(6 + ki)[:, 0:512], [psB[6 + ki]], [ktsB[ki]])
                for rr in range(4):
                    mm(ps[5][:, rr * 16:(rr + 1) * 16], kts[ki][:, rr * 128:(rr + 1) * 128], qr16[g], True, True,
                       [ktsB[ki], qrsB], [psB[5]])
                while pend2:
                    stepB(*pend2.pop(0))
                act(PT[i][:, 0:64], ps[5][:, 0:64], AF.Exp, [psB[5]], [PTB[i]], scale=SCALE)
                p3 = PT[i][:, 0:64].rearrange("p (r q) -> p r q", r=4)
                tt(p3, p3, maskB16[:, (r4 * 4) // 64, g, :].unsqueeze(1).to_broadcast([128, 4, 16]), ALU.mult,
                   [PTB[i], maskB16B], [PTB[i]])
                pend2.append((i, vi, g, r4))

        wtl = sb("wtl", [128, 4, 512], BF16); wtlB = Buf()
        S.dma("pool", S.newsem("c_wtl"), wtl[:], cwin.rearrange("(t p) c -> p t c", p=128), writes=[wtlB])
        vtw = sb("vtw", [128, 4, 2, 130], BF16); vtwB = Buf()
        S.op("dve", lambda e: e.memset(vtw[:, :, :, 128:130], 1.0), writes=[vtwB])
        for t4 in range(4):
            cp(vtw[:, t4, :, 0:128], wtl[:, t4, 256:512].rearrange("p (g d) -> p g d", g=2), [wtlB], [vtwB])
        for g in range(2):
            ki = g % 2
            for t4 in range(4):
                tr(psb16(6 + ki)[:, t4 * 128:(t4 + 1) * 128], wtl[:, t4, g * 128:(g + 1) * 128], identb[:],
                   [wtlB, identbB], [psB[6 + ki]])
            cp(kts[ki][:], psb16(6 + ki)[:, 0:512], [psB[6 + ki]], [ktsB[ki]])
            tiles = [dict(kt=kts[ki][:, t4 * 128:(t4 + 1) * 128], kR=[ktsB[ki]], va=vtw[:, t4, g, 0:129], vR=[vtwB], nk=128,
                          mask=(maskw16[:, t4, :], [maskw16B])) for t4 in range(4)]
            tiles.append(dict(kt=knew[:, 1, g, :], kR=[knewB], va=vnew[:, 1, g, 0:129], vR=[vnewB], nk=4,
                              mask=(maskn16[:, :], [maskn16B])))
            attend(1, 16, qr16[g], [qrsB], tiles, [obank[g] + 1])
            fin16(obank[g] + 1, g, 2)
        wo = [sb("wo0", [128, 16, 1024], BF16), sb("wo1", [128, 16, 1024], BF16)]; woB = [Buf(), Buf()]
        for hf in range(2):
            S.dma("pool", wtsem[hf], wo[hf][:], wo_t[hf], writes=[woB[hf]])
        gbc, gbcB = const("gbc4", gf, [128, D])
        junk = sb("junk4", [128, D], BF16)
        xs = [sb("xs0b", [128, D]), sb("xs1b", [128, D])]; xsB = [Buf(), Buf()]
        yb = [sb("yb0", [128, D]), sb("yb1", [128, D])]; ybB = [Buf(), Buf()]

        def outproj_slot(s, hook=None):
            j = s % 2
            npart = 128 if s < 8 else 4
            src = xall[s * 128:(s + 1) * 128, :] if s < 8 else xext[2:6, :]
            mR = [mixB] if s < 8 else [mixsB]
            S.dma("sp", xsem[j], xs[j][:npart, :], src, writes=[xsB[j]])
            for cgp in range(4):
                for kc in range(16):
                    lhsT = mixedT[:, kc, s * 128:(s + 1) * 128] if s < 8 else mixs[:, kc, :]
                    mm(ps[cgp][0:npart, :], lhsT, wo[cgp // 2][:, kc, (cgp % 2) * 512:(cgp % 2 + 1) * 512], kc == 0, kc == 15,
                       mR + [woB[cgp // 2]], [psB[cgp]])
                tt(yb[j][:npart, cgp * 512:(cgp + 1) * 512], ps[cgp][0:npart, :], xs[j][:npart, cgp * 512:(cgp + 1) * 512],
                   ALU.add, [psB[cgp], xsB[j]], [ybB[j]])
                if hook is not None:
                    hook()
            norm_stats(20 + s, yb[j][:npart, :], ybB[j], npart)
            stt(yb[j][:npart, :], yb[j][:npart, :], st[:npart, 20 + s, 3:4], gbc[:npart, :], ALU.mult, ALU.mult,
                [ybB[j], stB[20 + s], gbcB], [ybB[j]])
            dst = y_own[s * 128:(s + 1) * 128, :] if s < 8 else y_s[:, :]
            S.dma("sp", stsem[j], dst, yb[j][:npart, :], reads=[ybB[j]])

        p2n = [0]

        pass2_issue(0)

        def hook():
            if p2n[0] < 32:
                if p2n[0] + 1 < 32:
                    pass2_issue(p2n[0] + 1)
                pass2_step(p2n[0])
                p2n[0] += 1

        for s in range(8):
            outproj_slot(s, hook)
        while p2n[0] < 32:
            hook()
        while pend2:
            stepB(*pend2.pop(0))
        sbase[0] = 6
        for g in range(2):
            t = dict(kt=knew[:, 0, g, :], kR=[knewB], va=vnew[:, 0, g, 0:129], vR=[vnewB], nk=4, mask=(maskn16[:, :], [maskn16B]),
                     ooff=g * 130)
            att_tile(1, 16, qr16[g], [qrsB], t, [4], False, True)
            att_flush()
            fin16(4, g, 1, off=g * 130)
        onb16 = sb("onb16", [16, 128], BF16); onb16B = Buf()
        for g in range(2):
            cp(onb16[:], on16[g][:], [on16B[g]], [onb16B])
            tr(psb16(6)[:, 0:16], onb16[:, :], identb[0:16, 0:16], [onb16B, identbB], [psB[6]])
            tt(mixs[:, 4 + 4 * g:8 + 4 * g, :], psb16(6)[:, 0:16].rearrange("p (h t) -> p h t", h=4), arBs[:, 4 * g:4 * g + 4, :],
               ALU.mult, [psB[6], arBsB], [mixsB])
        outproj_slot(8)
        S.final_wait("sp")
        e5.close()
    return nc


_PROG = {}


def rope_tables(pos):
    half = 16
    freqs = np.power(np.float32(500000.0), -np.arange(half, dtype=np.float32) * np.float32(2.0 / 32)).astype(np.float32)
    ang = pos.astype(np.float32)[None, :] * freqs[:, None]
    c = np.cos(ang).astype(np.float32)
    s = np.sin(ang).astype(np.float32)
    return np.concatenate([c, c], 0), np.concatenate([-s, s], 0)


def kernel(x_prompt, x_sample, cache_kv, cache_win, state_conv, cache_mem, page_table, mem_prompt,
           norm_g, w_in, w_conv, a_cmp, w_cmp, mem_norm_g, w_mem_kv, w_out, final_g):
    f32 = np.float32
    x_prompt = np.asarray(x_prompt, f32); x_sample = np.asarray(x_sample, f32)
    w_in = np.asarray(w_in, f32)
    n_pool = cache_kv.shape[1]
    if "nc" not in _PROG:
        _PROG["nc"] = build_program(n_pool)
    nc = _PROG["nc"]

    blocks = block_list()
    cols = np.stack([(c + (np.arange(128) % 24 if k == "bg" else np.arange(128))) for (k, _, _, c) in blocks])
    w3 = w_in[0].reshape(16, 128, -1)
    win_t = np.ascontiguousarray(w3[:, :, cols].transpose(2, 1, 0, 3))
    wm_t = np.ascontiguousarray(np.asarray(w_mem_kv, f32)[0].reshape(16, 128, 4, 256).transpose(2, 1, 0, 3))
    wo_t = np.ascontiguousarray(np.asarray(w_out, f32)[0].reshape(16, 128, 2, 1024).transpose(2, 1, 0, 3))
    rep = lambda v: np.ascontiguousarray(np.broadcast_to(np.asarray(v, f32).reshape(1, -1), (128, D)))
    gn, gm, gf = rep(norm_g[0]), rep(mem_norm_g[0]), rep(final_g)
    identd = np.eye(128, dtype=f32)
    pmd = np.zeros((32, 32), f32)
    for m in range(32):
        pmd[(m + 16) % 32, m] = 1.0
    aTd = np.ascontiguousarray(np.asarray(a_cmp, f32)[0].transpose(2, 0, 1))
    wcd = np.ascontiguousarray(np.asarray(w_cmp, f32)[0].transpose(1, 0, 2))
    wcvd = np.ascontiguousarray(np.asarray(w_conv, f32)[0].reshape(3, 4, 128).transpose(2, 1, 0))
    trid = (np.arange(128)[:, None] <= np.arange(128)[None, :]).astype(f32)

    cache2 = np.ascontiguousarray(np.asarray(cache_kv, f32)[0]).reshape(n_pool * 256, 512)
    rowAd = (2.0 * np.arange(128, dtype=f32)).reshape(128, 1)
    colBd = np.ascontiguousarray(np.broadcast_to((2.0 * np.arange(128, dtype=f32) + 1.0)[None, :], (128, 128)))
    a0 = np.asarray(a_cmp, f32)[0]
    arepd = np.ascontiguousarray(np.broadcast_to(a0[:, np.arange(128) % 64, None, :], (2, 128, 2, 128)).transpose(1, 0, 2, 3)
                                 ).reshape(128, 512)
    ind2d = (np.arange(128)[:, None] // 64 == np.arange(2)[None, :]).astype(f32)
    rseld = (np.arange(16)[:, None] % 4 == np.arange(4)[None, :]).astype(f32)
    maskwd = ((np.arange(4)[None, :, None] * 128 + np.arange(128)[:, None, None]) > np.arange(4)[None, None, :]).astype(f32)
    masknd = (np.arange(4)[:, None] <= np.arange(4)[None, :]).astype(f32)
    maskw16d = np.ascontiguousarray(maskwd[:, :, np.arange(16) % 4])
    maskn16d = np.ascontiguousarray(masknd[:, np.arange(16) % 4])
    rselTd = np.ascontiguousarray(rseld.T)
    onehd = np.zeros((16, 6, 24), f32)
    for hh in range(4):
        for tt_ in range(4):
            for gg in range(2):
                for bb in range(3):
                    onehd[hh * 4 + tt_, gg * 3 + bb, (4 * gg + hh) * 3 + bb] = 1.0
    in_maps = []
    for c in range(8):
        b, par, i = c // 2, c % 2, c
        own = slice(par * 1024, (par + 1) * 1024)
        oth = slice((1 - par) * 1024, (2 - par) * 1024)
        xall = np.concatenate([x_prompt[b, own], x_prompt[b, oth]], 0)
        xext = np.zeros((NX, D), f32)
        if par == 1:
            xext[0:2] = x_prompt[b, 1022:1024]
        xext[2:6] = x_sample[i]
        pos = np.concatenate([np.arange(par * 1024, (par + 1) * 1024), np.arange((1 - par) * 1024, (2 - par) * 1024)])
        posx = np.concatenate([np.zeros(2), 16384 + np.arange(4), np.zeros(2)])
        cosT, sinT = rope_tables(np.concatenate([pos[:1024], posx, pos[1024:]]))
        winmd = np.zeros((128, 8, 5, 128), f32)
        cmpcd = np.zeros((128, 8, 3, 32), f32)
        blkabs = pos[::64] // 64
        for s in range(8):
            qpos = pos[s * 128:(s + 1) * 128]
            for t5, j in enumerate(win_tiles(s)):
                kpos = pos[j * 128:(j + 1) * 128]
                dlt = qpos[None, :] - kpos[:, None]
                winmd[:, s, t5, :] = ((dlt >= 0) & (dlt < 512)).astype(f32)
            valid = ((blkabs[None, :] + 1) * 64 <= qpos[:, None] + 1)
            cur = (qpos // 64)[:, None]
            forced = (blkabs[None, :] == 0) | (blkabs[None, :] == cur) | (blkabs[None, :] == cur - 1)
            fut = blkabs[None, :] > cur
            cmpcd[:, s, 0, :] = np.where(valid, 0.0, -1e5)
            cmpcd[:, s, 1, :] = (~forced & ~fut).astype(f32)
            cmpcd[:, s, 2, :] = np.where(fut, -1.0, np.where(forced, 1e9, 0.0))
        ealld = np.zeros((32, 16, 128), f32)
        for j in range(16):
            for kk in range(128):
                ealld[2 * j + kk // 64, j, kk] = 1.0 if j < 8 else float(par)
        stcd = np.ascontiguousarray(np.asarray(state_conv, f32)[0, i].reshape(2, 4, 128).transpose(2, 1, 0))
        pti = np.asarray(page_table)[i].astype(np.int32)
        in_maps.append(dict(
            xall=xall, xext=xext, gn=gn, gm=gm, gf=gf, win_t=win_t, wm_t=wm_t, wo_t=wo_t,
            memx=np.ascontiguousarray(np.asarray(mem_prompt, f32)[b]), cosT=cosT, sinT=sinT, identd=identd, pmd=pmd,
            aTd=aTd, wcd=wcd, wcvd=wcvd, stcd=stcd, trid=trid, winmd=winmd, cmpcd=cmpcd, ealld=ealld,
            cwin=np.ascontiguousarray(np.asarray(cache_win, f32)[0, i].reshape(512, 512)),
            cmem=np.ascontiguousarray(np.asarray(cache_mem, f32)[0, i].reshape(256, 1024)),
            cache2=cache2, ptrepd=np.ascontiguousarray(np.broadcast_to(pti[None, :], (128, 128))),
            ptcold=np.ascontiguousarray(pti.reshape(128, 1)), rowAd=rowAd, colBd=colBd, arepd=arepd, ind2d=ind2d, rseld=rseld,
            maskw16d=maskw16d, maskn16d=maskn16d, rselTd=rselTd, onehd=onehd,
        ))
    res = run_bass_kernel_spmd(nc, in_maps, core_ids=list(range(8)))
    R = res.results
    y_prompt = np.zeros((4, 2048, D), f32); kv_p = np.zeros((1, 4, 2048, 4, 2, 128), f32)
    for c in range(8):
        b, par = c // 2, c % 2
        y_prompt[b, par * 1024:(par + 1) * 1024] = R[c]["y_own"]
        kv_p[0, b, par * 1024:(par + 1) * 1024] = R[c]["kv_own"].reshape(1024, 4, 2, 128)
    y_sample = np.stack([R[c]["y_s"] for c in range(8)]).reshape(8, 4, D)
    win_p = np.stack([R[2 * b + 1]["win_own"] for b in range(4)]).reshape(1, 4, 512, 2, 2, 128)
    conv_pp = np.stack([R[2 * b + 1]["conv_p"] for b in range(4)]).reshape(1, 4, 2, 512)
    mem_p = np.stack([R[2 * b]["memkv"] for b in range(4)]).reshape(1, 4, 256, 2, 4, 128)
    kv_ss = np.stack([R[c]["kv_s"] for c in range(8)]).reshape(1, 8, 4, 4, 2, 128)
    win_ss = np.stack([R[c]["win_s"] for c in range(8)]).reshape(1, 8, 512, 2, 2, 128)
    conv_ss = np.stack([R[c]["conv_s"] for c in range(8)]).reshape(1, 8, 2, 512)
    return (y_prompt, y_sample, kv_p, win_p, conv_pp, mem_p, kv_ss, win_ss, conv_ss)


def win_tiles(s):
    own = [j for j in range(s - 4, s + 1) if j >= 0]
    oth = [8 + j for j in range(s + 4, 8)]
    return oth + own
```
